# Optimizing a Trainium2 kernel written in Bass

```python
import jax, jax.numpy as jnp
from jax import lax
import numpy as np

D_MODEL = 1024
BATCH = 4
SEQ = 8192
DEPTH = 1

HEAD_DIM = 64
N_Q_HEADS = 8
N_KV_HEADS = 2
Q_PER_KV = N_Q_HEADS // N_KV_HEADS
ATTN_WIDTH = N_Q_HEADS * HEAD_DIM
KV_WIDTH = N_KV_HEADS * HEAD_DIM
WINDOW = 128
BLOCK = 128
ROPE_THETA = 10000.0
CONV_WIDTH = D_MODEL - ATTN_WIDTH
CONV_KSIZE = 3
MIX_WIDTH = ATTN_WIDTH + CONV_WIDTH
IN_COLS = ATTN_WIDTH + 2 * KV_WIDTH + 3 * CONV_WIDTH
D_FF = 2816
FFN_RESIDUAL_WEIGHT = 0.5
MEM_LEN = 256
X_HEADS = 4
X_HEAD_DIM = D_MODEL // X_HEADS

RMS_EPS = 1e-5
NEG_INF = -1e30
MAX_POS_OFFSET = 4096

kernel_name = "hymba_swa_sink_shortconv_macaron_memxattn"


def rms_norm(x, gain):
    xf = x.astype(jnp.float32)
    y = xf * lax.rsqrt(jnp.mean(xf * xf, axis=-1, keepdims=True) + RMS_EPS)
    return (y * gain.astype(jnp.float32)).astype(x.dtype)


def swiglu(u, w_in, w_out):
    gate, up = jnp.split(u @ w_in, 2, axis=-1)
    return (jax.nn.silu(gate) * up) @ w_out


def rope(t, positions):
    half = HEAD_DIM // 2
    inv_freq = ROPE_THETA ** (-jnp.arange(half, dtype=jnp.float32) / half)
    ang = positions.astype(jnp.float32)[..., None] * inv_freq
    cos = jnp.cos(ang)[:, :, None, :]
    sin = jnp.sin(ang)[:, :, None, :]
    tf = t.astype(jnp.float32)
    t1, t2 = tf[..., :half], tf[..., half:]
    out = jnp.concatenate([t1 * cos - t2 * sin, t2 * cos + t1 * sin], axis=-1)
    return out.astype(t.dtype)


def sliding_window_attention(q, k, v, sinks):
    b, s = q.shape[0], q.shape[1]
    nb = s // BLOCK
    qb = q.reshape(b, nb, BLOCK, N_KV_HEADS, Q_PER_KV, HEAD_DIM)
    kb = k.reshape(b, nb, BLOCK, N_KV_HEADS, HEAD_DIM)
    vb = v.reshape(b, nb, BLOCK, N_KV_HEADS, HEAD_DIM)

    def band(t):
        prev = jnp.pad(t, ((0, 0), (1, 0), (0, 0), (0, 0), (0, 0)))[:, :-1]
        return jnp.concatenate([prev, t], axis=2)

    k_band, v_band = band(kb), band(vb)
    scores = jnp.einsum('bnqhgd,bnkhd->bnhgqk', qb, k_band).astype(jnp.float32) * (HEAD_DIM ** -0.5)
    qi = jnp.arange(BLOCK)[:, None]
    ki = jnp.arange(2 * BLOCK)[None, :]
    rel = BLOCK + qi - ki
    in_window = (rel >= 0) & (rel < WINDOW)
    blk = jnp.arange(nb)[:, None, None]
    valid = in_window[None] & ((blk > 0) | (ki[None] >= BLOCK))
    scores = jnp.where(valid[None, :, None, None], scores, NEG_INF)
    sink = sinks.astype(jnp.float32).reshape(N_KV_HEADS, Q_PER_KV)[None, None, :, :, None, None]
    sink = jnp.broadcast_to(sink, scores.shape[:-1] + (1,))
    probs = jax.nn.softmax(jnp.concatenate([scores, sink], axis=-1), axis=-1)[..., :-1]
    out = jnp.einsum('bnhgqk,bnkhd->bnqhgd', probs.astype(v.dtype), v_band)
    return out.reshape(b, s, ATTN_WIDTH)


def short_gated_conv(xc, gate_b, gate_c, conv_w):
    z = gate_c * xc
    rhs = conv_w[:, None, :].astype(z.dtype)
    conv = lax.conv_general_dilated(z, rhs, window_strides=(1,), padding=[(CONV_KSIZE - 1, 0)],
                                    dimension_numbers=('NWC', 'WIO', 'NWC'),
                                    feature_group_count=CONV_WIDTH)
    return gate_b * conv


def memory_cross_attention(u, mem_n, w_xq, w_xkv, w_xo):
    b, s, _ = u.shape
    m = mem_n.shape[1]
    q = (u @ w_xq).reshape(b, s, X_HEADS, X_HEAD_DIM)
    k, v = jnp.split(mem_n @ w_xkv, 2, axis=-1)
    k = k.reshape(b, m, X_HEADS, X_HEAD_DIM)
    v = v.reshape(b, m, X_HEADS, X_HEAD_DIM)
    scores = jnp.einsum('bshd,bmhd->bhsm', q, k).astype(jnp.float32) * (X_HEAD_DIM ** -0.5)
    probs = jax.nn.softmax(scores, axis=-1)
    o = jnp.einsum('bhsm,bmhd->bshd', probs.astype(v.dtype), v).reshape(b, s, D_MODEL)
    return o @ w_xo


def setup_inputs(seed: int = 0) -> dict:
    key = jax.random.key(seed)
    ks = jax.random.split(key, 24)
    f32 = jnp.float32

    def w(k, shape, fan_in):
        return jax.random.normal(k, shape, f32) * (fan_in ** -0.5)

    def gain(k, shape):
        return 1.0 + 0.02 * jax.random.normal(k, shape, f32)

    x = jax.random.normal(ks[0], (BATCH, SEQ, D_MODEL), f32)
    mem = jax.random.normal(ks[1], (BATCH, MEM_LEN, D_MODEL), f32)
    offsets = jax.random.randint(ks[2], (BATCH, 1), 0, MAX_POS_OFFSET, dtype=jnp.int32)
    positions = (jnp.arange(SEQ, dtype=jnp.int32)[None, :] + offsets).astype(jnp.int32)
    return {
        "x": x,
        "mem": mem,
        "positions": positions,
        "g_ffn1": gain(ks[3], (DEPTH, D_MODEL)),
        "w_ffn1_in": w(ks[4], (DEPTH, D_MODEL, 2 * D_FF), D_MODEL),
        "w_ffn1_out": w(ks[5], (DEPTH, D_FF, D_MODEL), D_FF),
        "g_mix": gain(ks[6], (DEPTH, D_MODEL)),
        "w_mix_in": w(ks[7], (DEPTH, D_MODEL, IN_COLS), D_MODEL),
        "sinks": 0.5 * jax.random.normal(ks[8], (DEPTH, N_Q_HEADS), f32),
        "conv_w": w(ks[9], (DEPTH, CONV_KSIZE, CONV_WIDTH), CONV_KSIZE),
        "g_attn_out": gain(ks[10], (DEPTH, ATTN_WIDTH)),
        "g_conv_out": gain(ks[11], (DEPTH, CONV_WIDTH)),
        "w_mix_out": w(ks[12], (DEPTH, MIX_WIDTH, D_MODEL), MIX_WIDTH),
        "g_mem": gain(ks[13], (DEPTH, D_MODEL)),
        "g_xattn": gain(ks[14], (DEPTH, D_MODEL)),
        "w_xq": w(ks[15], (DEPTH, D_MODEL, D_MODEL), D_MODEL),
        "w_xkv": w(ks[16], (DEPTH, D_MODEL, 2 * D_MODEL), D_MODEL),
        "w_xo": w(ks[17], (DEPTH, D_MODEL, D_MODEL), D_MODEL),
        "g_ffn2": gain(ks[18], (DEPTH, D_MODEL)),
        "w_ffn2_in": w(ks[19], (DEPTH, D_MODEL, 2 * D_FF), D_MODEL),
        "w_ffn2_out": w(ks[20], (DEPTH, D_FF, D_MODEL), D_FF),
        "g_final": gain(ks[21], (D_MODEL,)),
    }


def reference(x, mem, positions, g_ffn1, w_ffn1_in, w_ffn1_out, g_mix, w_mix_in, sinks, conv_w,
              g_attn_out, g_conv_out, w_mix_out, g_mem, g_xattn, w_xq, w_xkv, w_xo,
              g_ffn2, w_ffn2_in, w_ffn2_out, g_final):
    b, s, _ = x.shape
    h = x
    for l in range(DEPTH):
        h = h + FFN_RESIDUAL_WEIGHT * swiglu(rms_norm(h, g_ffn1[l]), w_ffn1_in[l], w_ffn1_out[l])

        u = rms_norm(h, g_mix[l])
        proj = u @ w_mix_in[l]
        o0 = ATTN_WIDTH
        o1 = o0 + KV_WIDTH
        o2 = o1 + KV_WIDTH
        o3 = o2 + CONV_WIDTH
        o4 = o3 + CONV_WIDTH
        q = rope(proj[..., :o0].reshape(b, s, N_Q_HEADS, HEAD_DIM), positions)
        k = rope(proj[..., o0:o1].reshape(b, s, N_KV_HEADS, HEAD_DIM), positions)
        v = proj[..., o1:o2].reshape(b, s, N_KV_HEADS, HEAD_DIM)
        gate_b = proj[..., o2:o3]
        gate_c = proj[..., o3:o4]
        xc = proj[..., o4:]

        attn = sliding_window_attention(q, k, v, sinks[l])
        conv = short_gated_conv(xc, gate_b, gate_c, conv_w[l])
        mixed = jnp.concatenate([rms_norm(attn, g_attn_out[l]), rms_norm(conv, g_conv_out[l])], axis=-1)
        h = h + mixed @ w_mix_out[l]

        h = h + memory_cross_attention(rms_norm(h, g_xattn[l]), rms_norm(mem, g_mem[l]),
                                       w_xq[l], w_xkv[l], w_xo[l])

        h = h + FFN_RESIDUAL_WEIGHT * swiglu(rms_norm(h, g_ffn2[l]), w_ffn2_in[l], w_ffn2_out[l])
    return rms_norm(h, g_final)
```

```python
import contextlib
import numpy as np
import concourse.bass as bass
import concourse.mybir as mybir
from concourse.bass_utils import run_bass_kernel_spmd

F32 = mybir.dt.float32
BF16 = mybir.dt.bfloat16
I32 = mybir.dt.int32
AF = mybir.ActivationFunctionType
ALU = mybir.AluOpType
AX = mybir.AxisListType

D = 1024
KC = 8
DFF = 2816
FC = 22
HALO = 128
MEM = 256
RMS_EPS = 1e-5
NEG = -30000.0
PI = float(np.pi)
TWO_PI = float(2 * np.pi)

GC_FFN1, GC_MIX, GC_XATT, GC_FFN2 = 0, 8, 16, 24
GC_ATT, GC_CONVG, GC_CONVW, GC_INVF, GC_SSCALE = 32, 36, 40, 52, 53
GCOLS = 56


class Cfg:
    def __init__(self, ntok=4096, T=1024, SUB=512):
        self.ntok, self.T, self.SUB = ntok, T, SUB
        self.npass = ntok // T
        self.nsub = T // SUB
        self.nblk = T // 128
        assert ntok % T == 0 and T % SUB == 0 and SUB % 128 == 0


def dsize(dt):
    return 2 if dt == BF16 else 4


class Op:
    __slots__ = ("eng", "fn", "deps", "sem", "val", "signals", "is_dma", "seq")
    _n = 0

    def __init__(self, eng, fn):
        self.eng, self.fn = eng, fn
        Op._n += 1
        self.seq = Op._n
        self.deps = []
        self.sem = None
        self.val = None
        self.signals = False
        self.is_dma = False


class Prog:
    CELL = 256
    ENGS = ("pe", "act", "dve", "pool", "sp")

    def __init__(self, nc):
        self.nc = nc
        self.ops = {e: [] for e in self.ENGS}
        self.bufs = {}
        self.cells = {}
        self.cache = {}
        self.dma_sems = {}
        self.dma_rr = {e: 0 for e in self.ENGS}
        self.sb_next = 16512
        self.sb_end = 229376

    def sb(self, name, shape, dtype, alias=None):
        size = int(np.prod(shape[1:])) * dsize(dtype)
        if alias is None:
            addr = (self.sb_next + 255) // 256 * 256
            self.sb_next = addr + size
            assert self.sb_next <= self.sb_end, f"SBUF overflow at {name}: {self.sb_next}"
        else:
            addr = alias
            assert addr % 32 == 0
        t = self.nc.alloc_sbuf_tensor_at(name, list(shape), dtype, offset=addr)
        self.bufs[t.name] = ("sb", addr, dsize(dtype))
        return t, addr, size

    def reg_psum(self, t):
        self.bufs[t.name] = ("ps", 0, 4)

    def _cells(self, ap):
        key = (ap.tensor.name, ap.offset, ap.ap)
        r = self.cache.get(key)
        if r is not None:
            return r
        space, base, ds = self.bufs[ap.tensor.name]
        dims = ap.ap
        pstep = dims[0][0]
        fo = ap.offset % pstep if pstep > 0 else ap.offset
        free = dims[1:]
        if not free:
            free = ((1, 1),)
        last_step, last_n = free[-1]
        starts = [fo]
        for (st, n) in free[:-1]:
            starts = [s + i * st for s in starts for i in range(n)]
        span = (last_n - 1) * abs(last_step) + 1
        csz = 2048 if space == "ps" else self.CELL
        cs = set()
        for s in starts:
            lo = base + s * ds
            hi = base + (s + span) * ds
            for c in range(lo // csz, (hi - 1) // csz + 1):
                cs.add((space, c))
        r = tuple(cs)
        self.cache[key] = r
        return r

    def _track(self, op, reads, writes):
        key = id(op) if op.is_dma else op.eng
        cands = {}

        def cand(other):
            if other is None or other is op:
                return
            if other.is_dma:
                cands[id(other)] = other
                return
            if other.eng == "pe" and op.eng == "pe" and not op.is_dma:
                return
            cur = cands.get(other.eng)
            if cur is None or cur.seq < other.seq:
                cands[other.eng] = other

        for ap in reads:
            for c in self._cells(ap):
                st = self.cells.get(c)
                if st is None:
                    st = [None, {}]
                    self.cells[c] = st
                cand(st[0])
                if c[0] == "ps":
                    for r in st[1].values():
                        if r.eng != op.eng:
                            cand(r)
                st[1][key] = op
        for ap in writes:
            for c in self._cells(ap):
                st = self.cells.get(c)
                if st is None:
                    st = [None, {}]
                    self.cells[c] = st
                cand(st[0])
                for r in st[1].values():
                    cand(r)
                st[0] = op
                st[1] = {}
        for other in cands.values():
            op.deps.append(other)
            other.signals = True

    def op(self, eng, fn, reads=(), writes=()):
        o = Op(eng, fn)
        self._track(o, reads, writes)
        self.ops[eng].append(o)
        return o

    def dma(self, eng, out, in_, reads=(), writes=()):
        o = Op(eng, None)
        o.is_dma = True
        pool = self.dma_sems[eng]
        slot = pool[self.dma_rr[eng] % len(pool)]
        self.dma_rr[eng] += 1
        if slot[2] is not None:
            o.deps.append(slot[2])
        slot[1] += 16
        slot[2] = o
        o.sem, o.val = slot[0], slot[1]
        o.signals = True
        o.fn = (out, in_)
        self._track(o, reads, writes)
        self.ops[eng].append(o)
        return o

    def finalize(self, sems):
        for e in self.ENGS:
            n = 0
            for o in self.ops[e]:
                if o.is_dma:
                    continue
                o.sem = sems[e]
                if o.signals:
                    n += 1
                    o.val = n

    def emit(self, eng, h, final_wait=()):
        seen = {}
        for o in self.ops[eng]:
            need = {}
            for d in o.deps:
                assert d.val is not None
                k = d.sem
                if seen.get(k.num, 0) >= d.val:
                    continue
                if need.get(k.num, (None, 0))[1] < d.val:
                    need[k.num] = (k, d.val)
            for num, (k, v) in need.items():
                h.wait_ge(k, v)
                seen[num] = v
            if o.is_dma:
                out, in_ = o.fn
                h.dma_start(out=out, in_=in_).then_inc(o.sem, 16)
            else:
                ins = o.fn(h)
                if o.signals:
                    ins.then_inc(o.sem, 1)
        for d in final_wait:
            if seen.get(d.sem.num, 0) < d.val:
                h.wait_ge(d.sem, d.val)
                seen[d.sem.num] = d.val


def build_program(cfg):
    nc = bass.Bass("TRN2", target_bir_lowering=False)
    P = Prog(nc)
    es = contextlib.ExitStack()
    sems = {e: es.enter_context(nc.semaphore(f's_{e}')) for e in Prog.ENGS}
    P.dma_sems = {'pool': [[es.enter_context(nc.semaphore(f'dp{i}')), 0, None] for i in range(16)],
                  'sp': [[es.enter_context(nc.semaphore(f'ds{i}')), 0, None] for i in range(8)]}
    T, SUB, NSUB, NBLK, NPASS = cfg.T, cfg.SUB, cfg.nsub, cfg.nblk, cfg.npass
    NTOT = HALO + cfg.ntok
    TP = T + 128

    def din(name, shape, dt=F32):
        return nc.dram_tensor(name, list(shape), dt, kind="ExternalInput")

    x_d = din("x", [NTOT, D])
    pos_d = din("pos", [128, NTOT], I32)
    mem_d = din("mem", [MEM, D])
    gains_d = din("gains", [128, GCOLS])
    gmem_d = din("gmem_bc", [128, D])
    gfin_d = din("gfin_bc", [128, D])
    sinks_d = din("sinks_bc", [128, 8])
    consts_d = din("consts", [128, 3, 128])
    masks_d = din("masks", [128, 2, 256])
    w1i_d = din("w_ffn1_in", [D, 2 * DFF])
    w1o_d = din("w_ffn1_out", [DFF, D])
    wmi_d = din("w_mix_in", [D, 2304])
    wmo_d = din("w_mix_out", [D, D])
    wxq_d = din("w_xq", [D, D])
    wxkv_d = din("w_xkv", [D, 2 * D])
    wxo_d = din("w_xo", [D, D])
    w2i_d = din("w_ffn2_in", [D, 2 * DFF])
    w2o_d = din("w_ffn2_out", [DFF, D])
    out_d = nc.dram_tensor("out", [cfg.ntok, D], F32, kind="ExternalOutput")

    h, _, _ = P.sb("h", [128, KC, T], F32)
    u, _, _ = P.sb("u", [128, KC, T], BF16)
    bufA, bufA_addr, bufA_size = P.sb("bufA", [128, KC, max(T, 1024)], BF16)
    hh, _, _ = P.sb("hh", [128, KC, HALO], F32, alias=bufA_addr)
    uh, _, _ = P.sb("uh", [128, KC, HALO], BF16, alias=bufA_addr + 4096)
    acth, _, _ = P.sb("acth", [128, FC, HALO], BF16, alias=bufA_addr + 4096 + 2048)
    assert 4096 + 2048 + FC * HALO * 2 <= bufA_size
    kA, _, _ = P.sb("kA", [128, TP], BF16)
    kB, _, _ = P.sb("kB", [128, TP], BF16)
    vtok, _, _ = P.sb("vtok", [128, NBLK + 1, 128], BF16)
    zprev, _, _ = P.sb("zprev", [128, 4, 2], F32)
    kx, _, _ = P.sb("kx", [128, KC, MEM], BF16)
    vx, _, _ = P.sb("vx", [128, 2, D], BF16)
    gains, _, _ = P.sb("gains", [128, GCOLS], F32)
    sink8, _, _ = P.sb("sink8", [128, 8], F32)
    identf, _, _ = P.sb("identf", [128, 128], F32)
    cbf, _, _ = P.sb("cbf", [128, 3, 128], BF16)
    masks, _, _ = P.sb("masks", [128, 2, 256], F32)
    epsc, _, _ = P.sb("epsc", [128, 1], F32)
    gfin, _, _ = P.sb("gfin", [128, D], F32)
    stage = [P.sb(f"stage{i}", [128, D], F32)[0] for i in range(2)]
    xstage = [P.sb(f"xstage{i}", [128, D], F32)[0] for i in range(2)]
    rstd = [P.sb(f"rstd{i}", [128, SUB], F32)[0] for i in range(2)]
    lnv = [P.sb(f"lnv{i}", [128, SUB], F32)[0] for i in range(2)]
    small = [P.sb(f"small{i}", [128, 4], F32)[0] for i in range(8)]
    sgb = [P.sb(f"sgb{i}", [128, SUB], F32)[0] for i in range(2)]
    tmpb = [P.sb(f"tmpb{i}", [128, max(SUB, 512)], BF16)[0] for i in range(2)]
    act, scr_addr, act_size = P.sb("act", [128, FC, T], BF16)
    a = [scr_addr]

    def scr(name, shape, dt):
        t_, _, sz = P.sb(name, shape, dt, alias=a[0])
        a[0] += (sz + 255) // 256 * 256
        return t_

    qr = scr("qr", [128, 4, T], BF16)
    pT = [scr(f"pT{i}", [128, 2, 512], BF16) for i in range(2)]
    zb2 = [scr(f"zb{i}", [128, SUB + 2], F32) for i in range(2)]
    yb_addr = a[0]
    ybs = [scr(f"yb{i}", [128, 4, SUB], F32) for i in range(NSUB)]
    if 4 * 1280 <= NSUB * 4 * SUB * 4:
        pf = [P.sb(f"pf{i}", [128, 264], F32, alias=yb_addr + i * 1280)[0] for i in range(4)]
    else:
        pf = [scr(f"pf{i}", [128, 264], F32) for i in range(4)]
    ssb = scr("ssb", [128, 8, 264], F32)
    assert a[0] <= scr_addr + act_size or T < 1024, (a[0] - scr_addr, act_size)
    a[0] = max(a[0], scr_addr + act_size)
    cos2 = scr("cos2", [128, TP], F32)
    sinS = scr("sinS", [128, TP], F32)
    tmpf_addr = a[0]
    tmpf = [scr(f"tmpf{i}", [128, SUB], F32) for i in range(3)]
    scr_end = a[0]
    RCH = ((3 * SUB) // 256 * 256) // 4
    a[0] = tmpf_addr
    posi = scr("posi", [128, RCH], I32)
    posf = scr("posf", [128, RCH], F32)
    ang = scr("ang", [128, RCH], F32)
    kf = scr("kf", [128, RCH], F32)
    assert a[0] <= scr_end
    P.sb_next = max(P.sb_next, scr_end)
    memt = [P.sb(f"memt{i}", [128, D], F32, alias=scr_addr + i * 4096)[0] for i in range(2)]
    memn = [P.sb(f"memn{i}", [128, D], F32, alias=scr_addr + 8192 + i * 4096)[0] for i in range(2)]
    memT, _, _ = P.sb("memT", [128, KC, MEM], BF16, alias=scr_addr + 16384)
    cstage, _, _ = P.sb("cstage", [128, 3, 128], F32, alias=scr_addr + 16384 + 4096)
    ring_addr = (P.sb_next + 255) // 256 * 256
    ring_size = (P.sb_end - ring_addr) // 256 * 256
    assert ring_size >= 24 * 1024, f"ring too small {ring_size}"
    ring_pos = [0]
    wcount = [0]

    def wtile(nfree):
        size = (nfree * 2 + 255) // 256 * 256
        if ring_pos[0] + size > ring_size:
            ring_pos[0] = 0
        addr = ring_addr + ring_pos[0]
        ring_pos[0] += size
        wcount[0] += 1
        t_ = nc.alloc_sbuf_tensor_at(f"wt{wcount[0]}", [128, nfree], BF16, offset=addr)
        P.bufs[t_.name] = ("sb", addr, 2)
        return t_

    ps = nc.alloc_psum_tensor("ps", [128, 8, 512], F32)
    P.reg_psum(ps)
    bank_rr = [0]

    def bank():
        b = bank_rr[0] % 6
        bank_rr[0] += 1
        return b

    def wdma(dst, src):
        return P.dma("pool", dst, src, writes=[dst])

    def ldma(dst, src):
        return P.dma("sp", dst, src, writes=[dst])

    def mm(out, lhsT, rhs, start, stop):
        return P.op("pe", lambda e: e.matmul(out, lhsT=lhsT, rhs=rhs, start=start, stop=stop),
                    reads=[lhsT, rhs], writes=[out])

    def tr(out, in_, ident):
        return P.op("pe", lambda e: e.transpose(out, in_, ident), reads=[in_, ident], writes=[out])

    def act_fn(out, in_, func, scale=1.0, bias=None, accum=None):
        reads = [in_]
        kw = {}
        if bias is not None:
            kw["bias"] = bias
            reads.append(bias)
        if not isinstance(scale, float):
            reads.append(scale)
        writes = [out]
        if accum is not None:
            kw["accum_out"] = accum
            writes.append(accum)
        return P.op("act", lambda e: e.activation(out=out, in_=in_, func=func, scale=scale, **kw),
                    reads=reads, writes=writes)

    def v_tt(out, in0, in1, op):
        return P.op("dve", lambda e: e.tensor_tensor(out=out, in0=in0, in1=in1, op=op),
                    reads=[in0, in1], writes=[out])

    def v_ts(out, in0, s1, op0, s2=None, op1=None):
        reads = [in0] + [s for s in (s1, s2) if s is not None and not isinstance(s, float)]
        if op1 is None:
            return P.op("dve", lambda e: e.tensor_scalar(out=out, in0=in0, scalar1=s1, scalar2=None, op0=op0),
                        reads=reads, writes=[out])
        return P.op("dve", lambda e: e.tensor_scalar(out=out, in0=in0, scalar1=s1, scalar2=s2, op0=op0, op1=op1),
                    reads=reads, writes=[out])

    def v_stt(out, in0, scalar, in1, op0, op1):
        reads = [in0, in1] + ([] if isinstance(scalar, float) else [scalar])
        return P.op("dve", lambda e: e.scalar_tensor_tensor(out=out, in0=in0, scalar=scalar, in1=in1, op0=op0, op1=op1),
                    reads=reads, writes=[out])

    def v_copy(out, in_, eng="dve"):
        if eng == "act":
            return P.op("act", lambda e: e.copy(out=out, in_=in_), reads=[in_], writes=[out])
        return P.op("dve", lambda e: e.tensor_copy(out=out, in_=in_), reads=[in_], writes=[out])

    def v_max(out, in_):
        return P.op("dve", lambda e: e.tensor_reduce(out=out, in_=in_, axis=AX.X, op=ALU.max),
                    reads=[in_], writes=[out])

    def v_recip(out, in_):
        return P.op("dve", lambda e: e.reciprocal(out=out, in_=in_), reads=[in_], writes=[out])

    def v_memset(ap, val):
        return P.op("dve", lambda e: e.memset(ap, val), writes=[ap])

    def wsrc(wd, r0, nr, c0, ncol):
        return wd.ap()[r0:r0 + nr * 128, c0:c0 + ncol].rearrange("(kc p) n -> p kc n", p=128)

    ones = cbf[:, 0, :]
    p32 = cbf[:, 1, :]
    p64 = cbf[:, 2, :]
    rr = {}

    def nxt(key, lst):
        i = rr.get(key, 0)
        rr[key] = i + 1
        return lst[i % len(lst)]

    ldma(gains[:, :], gains_d.ap())
    ldma(sink8[:, :], sinks_d.ap())
    ldma(cstage[:, :, :], consts_d.ap())
    ldma(masks[:, :, :], masks_d.ap())
    ldma(gfin[:, :], gfin_d.ap())
    v_memset(epsc[:, :], RMS_EPS)
    v_memset(cbf[:, 0, :], 1.0)
    v_copy(identf[:, :], cstage[:, 0, :])
    v_copy(cbf[:, 1, :], cstage[:, 1, :])
    v_copy(cbf[:, 2, :], cstage[:, 2, :])
    v_memset(zprev[:, :, :], 0.0)

    def rms_stats_fm(src_chunks, n, nfeat):
        b = bank()
        ssp = ps[:, b, 0:n]
        nchunk = len(src_chunks)
        for i, s in enumerate(src_chunks):
            sq = nxt("tmpb", tmpb)[:, 0:n]
            act_fn(sq, s, AF.Square)
            mm(ssp, ones, sq, i == 0, i == nchunk - 1)
        i_ = rr.get("rstd", 0)
        rr["rstd"] = i_ + 1
        lv = lnv[i_ % 2][:, 0:n]
        rs = rstd[i_ % 2][:, 0:n]
        act_fn(lv, ssp, AF.Ln, scale=1.0 / nfeat, bias=epsc[:, :])
        act_fn(rs, lv, AF.Exp, scale=-0.5)
        return rs

    def rmsnorm_to_u(hsrc, udst, n, gcol):
        rs = rms_stats_fm([hsrc[:, kc, :] for kc in range(KC)], n, D)
        for kc in range(KC):
            v_stt(udst[:, kc, :], hsrc[:, kc, :], gains[:, gcol + kc:gcol + kc + 1], rs, ALU.mult, ALU.mult)

    ldma(memt[0][:, :], mem_d.ap()[0:128, :])
    ldma(memt[1][:, :], mem_d.ap()[128:256, :])
    gmem_sb = stage[0]
    ldma(gmem_sb[:, :], gmem_d.ap())
    for mb in range(2):
        ssq = small[0][:, mb:mb + 1]
        act_fn(stage[1][:, :], memt[mb][:, :], AF.Square, accum=ssq)
        lv_ = small[1][:, mb:mb + 1]
        act_fn(lv_, ssq, AF.Ln, scale=1.0 / D, bias=epsc[:, :])
        rs_ = small[2][:, mb:mb + 1]
        act_fn(rs_, lv_, AF.Exp, scale=-0.5)
        v_stt(memn[mb][:, :], memt[mb][:, :], rs_, gmem_sb[:, :], ALU.mult, ALU.mult)
        for half in range(2):
            b = bank()
            for q in range(4):
                kc = half * 4 + q
                tr(ps[:, b, q * 128:(q + 1) * 128], memn[mb][:, kc * 128:(kc + 1) * 128], identf[:, :])
            v_copy(memT[:, half * 4:half * 4 + 4, mb * 128:(mb + 1) * 128],
                   ps[:, b, :].rearrange("p (a c) -> p a c", a=4), eng=("act" if half else "dve"))
    for ti in range(4):
        wt = wtile(KC * 256)
        wv = wt[:, :].rearrange("p (k n) -> p k n", k=KC)
        wdma(wv, wsrc(wxkv_d, 0, KC, ti * 256, 256))
        for cc in range(2):
            oc = ti * 2 + cc
            b = bank()
            for kc in range(KC):
                mm(ps[:, b, 0:MEM], wv[:, kc, cc * 128:(cc + 1) * 128], memT[:, kc, :], kc == 0, kc == KC - 1)
            v_copy(kx[:, oc, :], ps[:, b, 0:MEM], eng=("act" if cc else "dve"))
    for ti in range(2):
        wt = wtile(KC * 512)
        wv = wt[:, :].rearrange("p (k n) -> p k n", k=KC)
        wdma(wv, wsrc(wxkv_d, 0, KC, D + ti * 512, 512))
        for mb in range(2):
            b = bank()
            for kc in range(KC):
                mm(ps[:, b, :], memT[:, kc, mb * 128:(mb + 1) * 128], wv[:, kc, :], kc == 0, kc == KC - 1)
            v_copy(vx[:, mb, ti * 512:(ti + 1) * 512], ps[:, b, :], eng=("act" if mb else "dve"))

    def load_x(row0, nblocks, hdst, col_blk0=0):
        for bi_ in range(nblocks):
            bi = bi_ + col_blk0
            row0_ = row0 - col_blk0 * 128
            stg = nxt("xstage", xstage)
            ldma(stg[:, :], x_d.ap()[row0_ + bi * 128: row0_ + (bi + 1) * 128, :])
            for half in range(2):
                b = bank()
                for q in range(4):
                    kc = half * 4 + q
                    tr(ps[:, b, q * 128:(q + 1) * 128], stg[:, kc * 128:(kc + 1) * 128], identf[:, :])
                v_copy(hdst[:, half * 4:half * 4 + 4, bi * 128:(bi + 1) * 128],
                       ps[:, b, :].rearrange("p (a c) -> p a c", a=4), eng=("act" if half else "dve"))

    def ffn(subs, gcol, wi_d, wo_d, hook=None):
        for (hs, us, as_, n) in subs:
            rmsnorm_to_u(hs, us, n, gcol)
        if hook is not None:
            hook()
        for ti in range(FC // 2):
            wt = wtile(KC * 512)
            wv = wt[:, :].rearrange("p (k n) -> p k n", k=KC)
            wdma(wv[:, :, 0:256], wsrc(wi_d, 0, KC, ti * 256, 256))
            wdma(wv[:, :, 256:512], wsrc(wi_d, 0, KC, DFF + ti * 256, 256))
            for cc in range(2):
                c = ti * 2 + cc
                for (hs, us, as_, n) in subs:
                    bg, bu = bank(), bank()
                    for kc in range(KC):
                        mm(ps[:, bg, 0:n], wv[:, kc, cc * 128:(cc + 1) * 128], us[:, kc, :], kc == 0, kc == KC - 1)
                    for kc in range(KC):
                        mm(ps[:, bu, 0:n], wv[:, kc, 256 + cc * 128:256 + (cc + 1) * 128], us[:, kc, :], kc == 0, kc == KC - 1)
                    sg = nxt("sgb", sgb)[:, 0:n]
                    act_fn(sg, ps[:, bg, 0:n], AF.Silu)
                    v_tt(as_[:, c, :], sg, ps[:, bu, 0:n], ALU.mult)
        for tj in range(4):
            wt = wtile(FC * 256)
            wv = wt[:, :].rearrange("p (k n) -> p k n", k=FC)
            wdma(wv[:, 0:11, :], wsrc(wo_d, 0, 11, tj * 256, 256))
            wdma(wv[:, 11:22, :], wsrc(wo_d, 11 * 128, 11, tj * 256, 256))
            for mmi in range(2):
                m = tj * 2 + mmi
                for (hs, us, as_, n) in subs:
                    b = bank()
                    for kc in range(FC):
                        mm(ps[:, b, 0:n], wv[:, kc, mmi * 128:(mmi + 1) * 128], as_[:, kc, :], kc == 0, kc == FC - 1)
                    v_stt(hs[:, m, :], ps[:, b, 0:n], 0.5, hs[:, m, :], ALU.mult, ALU.add)

    def rope_tables(col0, ncols, pos_col0):
        for o in range(0, ncols, RCH):
            w = min(RCH, ncols - o)
            sl = slice(col0 + o, col0 + o + w)
            tl = slice(0, w)
            ldma(posi[:, tl], pos_d.ap()[:, pos_col0 + o:pos_col0 + o + w])
            v_copy(posf[:, tl], posi[:, tl])
            v_ts(ang[:, tl], posf[:, tl], gains[:, GC_INVF:GC_INVF + 1], ALU.mult)
            for which in range(2):
                dst = sinS if which == 0 else cos2
                src_ = ang
                if which == 1:
                    v_ts(posf[:, tl], ang[:, tl], PI / 2, ALU.add)
                    src_ = posf
                v_ts(kf[:, tl], src_[:, tl], 1.0 / TWO_PI, ALU.mult)
                v_copy(posi[:, tl], kf[:, tl])
                v_copy(kf[:, tl], posi[:, tl])
                C1 = 6.28125
                C2 = TWO_PI - C1
                v_stt(dst[:, sl], kf[:, tl], -C1, src_[:, tl], ALU.mult, ALU.add)
                v_stt(dst[:, sl], kf[:, tl], -C2, dst[:, sl], ALU.mult, ALU.add)
                v_ts(kf[:, tl], dst[:, sl], PI, ALU.is_gt)
                v_stt(dst[:, sl], kf[:, tl], -TWO_PI, dst[:, sl], ALU.mult, ALU.add)
                v_ts(kf[:, tl], dst[:, sl], -PI, ALU.is_lt)
                v_stt(dst[:, sl], kf[:, tl], TWO_PI, dst[:, sl], ALU.mult, ALU.add)
                v_ts(dst[:, sl], dst[:, sl], PI, ALU.min, -PI, ALU.max)
                if which == 0:
                    act_fn(dst[:, sl], dst[:, sl], AF.Sin, scale=gains[:, GC_SSCALE:GC_SSCALE + 1])
                else:
                    act_fn(dst[:, sl], dst[:, sl], AF.Sin)

    def rope_apply(psrc_bank, n, tcol0, dst_ap):
        qb = nxt("tmpb", tmpb)[:, 0:n]
        v_copy(qb, ps[:, psrc_bank, 0:n], eng="act")
        b2 = bank()
        mm(ps[:, b2, 0:n], p32, qb, True, True)
        t1 = nxt("tmpf", tmpf)[:, 0:n]
        v_tt(t1, ps[:, psrc_bank, 0:n], cos2[:, tcol0:tcol0 + n], ALU.mult)
        t2 = nxt("tmpf", tmpf)[:, 0:n]
        v_tt(t2, ps[:, b2, 0:n], sinS[:, tcol0:tcol0 + n], ALU.mult)
        v_tt(dst_ap, t1, t2, ALU.add)

    def mixer(p_idx, main_subs, halo):
        if halo:
            rmsnorm_to_u(hh, uh, HALO, GC_MIX)
        for (c0, n) in main_subs:
            rmsnorm_to_u(h[:, :, c0:c0 + n], u[:, :, c0:c0 + n], n, GC_MIX)
        toks = []
        if halo:
            toks.append((uh, HALO, 0, True, None))
        for (c0, n) in main_subs:
            toks.append((u[:, :, c0:c0 + n], n, 128 + c0, False, c0))
        for ti in range(2):
            wt = wtile(KC * 256)
            wv = wt[:, :].rearrange("p (k n) -> p k n", k=KC)
            wdma(wv, wsrc(wmi_d, 0, KC, ti * 256, 256))
            for cc in range(2):
                c = ti * 2 + cc
                for (us, n, tc0, ish, c0) in toks:
                    if ish:
                        continue
                    b = bank()
                    for kc in range(KC):
                        mm(ps[:, b, 0:n], wv[:, kc, cc * 128:(cc + 1) * 128], us[:, kc, :], kc == 0, kc == KC - 1)
                    rope_apply(b, n, tc0, qr[:, c, c0:c0 + n])
        wt = wtile(KC * 256)
        wv = wt[:, :].rearrange("p (k n) -> p k n", k=KC)
        wdma(wv, wsrc(wmi_d, 0, KC, 512, 256))
        for (us, n, tc0, ish, c0) in toks:
            b = bank()
            for kc in range(KC):
                mm(ps[:, b, 0:n], wv[:, kc, 0:128], us[:, kc, :], kc == 0, kc == KC - 1)
            rope_apply(b, n, tc0, kA[:, tc0:tc0 + n])
            b3 = bank()
            mm(ps[:, b3, 0:n], p64, kA[:, tc0:tc0 + n], True, True)
            v_copy(kB[:, tc0:tc0 + n], ps[:, b3, 0:n], eng="act")
            b4 = bank()
            nb_ = n // 128
            for bi in range(nb_):
                for kc in range(KC):
                    mm(ps[:, b4, bi * 128:(bi + 1) * 128], us[:, kc, bi * 128:(bi + 1) * 128], wv[:, kc, 128:256],
                       kc == 0, kc == KC - 1)
            blk0 = tc0 // 128
            v_copy(vtok[:, blk0:blk0 + nb_, :], ps[:, b4, 0:n].rearrange("p (a c) -> p a c", a=nb_))
        for c in range(4):
            wt = wtile(KC * 384)
            wv = wt[:, :].rearrange("p (k n) -> p k n", k=KC)
            wdma(wv[:, :, 0:128], wsrc(wmi_d, 0, KC, 768 + c * 128, 128))
            wdma(wv[:, :, 128:256], wsrc(wmi_d, 0, KC, 1280 + c * 128, 128))
            wdma(wv[:, :, 256:384], wsrc(wmi_d, 0, KC, 1792 + c * 128, 128))
            for (us, n, tc0, ish, c0) in toks:
                bgc, bxc = bank(), bank()
                for kc in range(KC):
                    mm(ps[:, bgc, 0:n], wv[:, kc, 128:256], us[:, kc, :], kc == 0, kc == KC - 1)
                for kc in range(KC):
                    mm(ps[:, bxc, 0:n], wv[:, kc, 256:384], us[:, kc, :], kc == 0, kc == KC - 1)
                xs = nxt("tmpf", tmpf)[:, 0:n]
                v_copy(xs, ps[:, bxc, 0:n], eng="act")
                if ish:
                    zt = nxt("tmpf", tmpf)[:, 0:n]
                    v_tt(zt, ps[:, bgc, 0:n], xs, ALU.mult)
                    v_copy(zprev[:, c, :], zt[:, n - 2:n])
                    continue
                bgb = bank()
                for kc in range(KC):
                    mm(ps[:, bgb, 0:n], wv[:, kc, 0:128], us[:, kc, :], kc == 0, kc == KC - 1)
                sidx = c0 // SUB
                zz = nxt("zb", zb2)
                v_copy(zz[:, 0:2], zprev[:, c, :])
                v_tt(zz[:, 2:2 + n], ps[:, bgc, 0:n], xs, ALU.mult)
                v_copy(zprev[:, c, :], zz[:, n:n + 2])
                cv = nxt("tmpf", tmpf)[:, 0:n]
                gw = GC_CONVW + c * 3
                v_ts(cv, zz[:, 2:2 + n], gains[:, gw + 2:gw + 3], ALU.mult)
                v_stt(cv, zz[:, 1:1 + n], gains[:, gw + 1:gw + 2], cv, ALU.mult, ALU.add)
                v_stt(cv, zz[:, 0:n], gains[:, gw:gw + 1], cv, ALU.mult, ALU.add)
                v_tt(ybs[sidx][:, c, 0:n], cv, ps[:, bgb, 0:n], ALU.mult)
                if c == 3:
                    rs = rms_stats_fm([ybs[sidx][:, cc_, 0:n] for cc_ in range(4)], n, 512)
                    for cc_ in range(4):
                        v_stt(bufA[:, 4 + cc_, c0:c0 + n], ybs[sidx][:, cc_, 0:n],
                              gains[:, GC_CONVG + cc_:GC_CONVG + cc_ + 1], rs, ALU.mult, ALU.mult)
        for hd in range(8):
            v_copy(ssb[:, hd, 256:257], sink8[:, hd:hd + 1])
        items = [(bi, g, j) for bi in range(NBLK) for g in range(2) for j in range(4)]
        st_ = {}

        def a1(it):
            bi, g, j = it
            hd = 4 * g + j
            c = hd // 2
            half = hd % 2
            pl = slice(half * 64, half * 64 + 64)
            kk = kA if half == g else kB
            gblk = p_idx * NBLK + bi
            mk = masks[:, 0, :] if gblk == 0 else masks[:, 1, :]
            b = bank()
            mm(ps[:, b, 0:256], qr[pl, c, bi * 128:(bi + 1) * 128], kk[pl, bi * 128:bi * 128 + 256], True, True)
            v_stt(ssb[:, hd, 0:256], ps[:, b, 0:256], 0.125, mk, ALU.mult, ALU.add)
            st_[it] = [nxt("small", small), nxt("pf", pf)]

        def a2(it):
            bi, g, j = it
            hd = 4 * g + j
            sm = st_[it][0]
            P.op("dve", lambda e: e.tensor_reduce(out=sm[:, 0:1], in_=ssb[:, hd, 0:257], axis=AX.X, op=ALU.max, negate=True),
                 reads=[ssb[:, hd, 0:257]], writes=[sm[:, 0:1]])

        def a3(it):
            bi, g, j = it
            hd = 4 * g + j
            sm, pfb = st_[it]
            act_fn(pfb[:, 0:257], ssb[:, hd, 0:257], AF.Exp, bias=sm[:, 0:1], accum=sm[:, 2:3])

        def b1(it):
            sm, pfb = st_[it]
            v_recip(sm[:, 3:4], sm[:, 2:3])

        def b2(it):
            sm, pfb = st_[it]
            v_ts(pfb[:, 0:256], pfb[:, 0:256], sm[:, 3:4], ALU.mult)

        def b3(it):
            bi, g, j = it
            sm, pfb = st_.pop(it)
            if j == 0:
                st_[("pT", bi, g)] = nxt("pT", pT)
            pTg = st_[("pT", bi, g)]
            bt = bank()
            for kb in range(2):
                tr(ps[:, bt, kb * 128:(kb + 1) * 128], pfb[:, kb * 128:(kb + 1) * 128], identf[:, :])
            v_copy(pTg[:, :, j * 128:(j + 1) * 128], ps[:, bt, 0:256].rearrange("p (a c) -> p a c", a=2), eng="act")
            gblk = p_idx * NBLK + bi
            bo = 6 + (gblk % 2)
            if j == 3:
                for kb in range(2):
                    mm(ps[g * 64:(g + 1) * 64, bo, :], vtok[:, bi + kb, g * 64:(g + 1) * 64], pTg[:, kb, :],
                       kb == 0, kb == 1)
                del st_[("pT", bi, g)]
            if j == 3 and g == 1:
                qcols = slice(bi * 128, (bi + 1) * 128)
                sqa = nxt("tmpb", tmpb)
                act_fn(sqa[:, 0:512], ps[:, bo, :], AF.Square)
                bs_ = bank()
                for jj in range(4):
                    mm(ps[:, bs_, 0:128], ones, sqa[:, jj * 128:(jj + 1) * 128], jj == 0, jj == 3)
                i_ = rr.get("rstd", 0)
                rr["rstd"] = i_ + 1
                lv = lnv[i_ % 2][:, 0:128]
                rs = rstd[i_ % 2][:, 0:128]
                act_fn(lv, ps[:, bs_, 0:128], AF.Ln, scale=1.0 / 512, bias=epsc[:, :])
                act_fn(rs, lv, AF.Exp, scale=-0.5)
                st_[("nrm", bi)] = (bo, rs, qcols)

        def b4(bi):
            bo, rs, qcols = st_.pop(("nrm", bi))
            for jj in range(4):
                v_stt(bufA[:, jj, qcols], ps[:, bo, jj * 128:(jj + 1) * 128],
                      gains[:, GC_ATT + jj:GC_ATT + jj + 1], rs, ALU.mult, ALU.mult)

        DEPTH = 3
        nit = len(items)
        for idx in range(nit + DEPTH + 1):
            ia = items[idx] if idx < nit else None
            ib = items[idx - DEPTH] if 0 <= idx - DEPTH < nit else None
            if ia:
                a1(ia)
            if ib:
                b1(ib)
            if ia:
                a2(ia)
            if ib:
                b2(ib)
            if ia:
                a3(ia)
            if ib:
                b3(ib)
            il = idx - DEPTH - 1
            if 0 <= il < nit and items[il][1] == 1 and items[il][2] == 3:
                b4(items[il][0])
        assert not st_, st_.keys()
        for tj in range(4):
            wt = wtile(KC * 256)
            wv = wt[:, :].rearrange("p (k n) -> p k n", k=KC)
            for g in range(2):
                src = wmo_d.ap()[g * 256:(g + 1) * 256, tj * 256:(tj + 1) * 256].rearrange("(j d) n -> d j n", d=64)
                wdma(wv[g * 64:(g + 1) * 64, 0:4, :], src)
            wdma(wv[:, 4:8, :], wsrc(wmo_d, 512, 4, tj * 256, 256))
            for mmi in range(2):
                m = tj * 2 + mmi
                for (c0, n) in main_subs:
                    b = bank()
                    for kc in range(KC):
                        mm(ps[:, b, 0:n], wv[:, kc, mmi * 128:(mmi + 1) * 128], bufA[:, kc, c0:c0 + n], kc == 0, kc == KC - 1)
                    v_tt(h[:, m, c0:c0 + n], ps[:, b, 0:n], h[:, m, c0:c0 + n], ALU.add)

    def xattn(main_subs):
        for (c0, n) in main_subs:
            rmsnorm_to_u(h[:, :, c0:c0 + n], u[:, :, c0:c0 + n], n, GC_XATT)
        for tj in range(4):
            wt = wtile(KC * 256)
            wv = wt[:, :].rearrange("p (k n) -> p k n", k=KC)
            wdma(wv, wsrc(wxq_d, 0, KC, tj * 256, 256))
            for mmi in range(2):
                m = tj * 2 + mmi
                for (c0, n) in main_subs:
                    b = bank()
                    for kc in range(KC):
                        mm(ps[:, b, 0:n], wv[:, kc, mmi * 128:(mmi + 1) * 128], u[:, kc, c0:c0 + n], kc == 0, kc == KC - 1)
                    v_copy(bufA[:, m, c0:c0 + n], ps[:, b, 0:n], eng=("act" if mmi else "dve"))
        items = [(c0, n, hx, bi) for (c0, n) in main_subs for hx in range(4) for bi in range(n // 128)]
        st_ = {}

        def xa1(it):
            c0, n, hx, bi = it
            qc = slice(c0 + bi * 128, c0 + (bi + 1) * 128)
            b = bank()
            for dc in range(2):
                mm(ps[:, b, 0:MEM], bufA[:, 2 * hx + dc, qc], kx[:, 2 * hx + dc, :], dc == 0, dc == 1)
            sm = nxt("small", small)
            pfb = nxt("pf", pf)
            st_[it] = (sm, pfb, b)
            P.op("dve", lambda e: e.tensor_reduce(out=sm[:, 0:1], in_=ps[:, b, 0:MEM], axis=AX.X, op=ALU.max, negate=True),
                 reads=[ps[:, b, 0:MEM]], writes=[sm[:, 0:1]])

        def xa2(it):
            sm, pfb, b = st_[it]
            v_ts(sm[:, 1:2], sm[:, 0:1], 1.0 / 16, ALU.mult)

        def xa3(it):
            sm, pfb, b = st_[it]
            act_fn(pfb[:, 0:MEM], ps[:, b, 0:MEM], AF.Exp, scale=1.0 / 16, bias=sm[:, 1:2], accum=sm[:, 2:3])

        def xb1(it):
            sm, pfb, b = st_[it]
            v_recip(sm[:, 3:4], sm[:, 2:3])

        def xb2(it):
            sm, pfb, b = st_[it]
            v_ts(pfb[:, 0:MEM], pfb[:, 0:MEM], sm[:, 3:4], ALU.mult)

        def xb3(it):
            c0, n, hx, bi = it
            sm, pfb, b = st_.pop(it)
            if bi == 0:
                st_[("pT", c0, hx)] = nxt("pT", pT)
            pTx = st_[("pT", c0, hx)]
            bt = bank()
            for mb in range(2):
                tr(ps[:, bt, mb * 128:(mb + 1) * 128], pfb[:, mb * 128:(mb + 1) * 128], identf[:, :])
            v_copy(pTx[:, :, bi * 128:(bi + 1) * 128], ps[:, bt, 0:256].rearrange("p (a c) -> p a c", a=2), eng="act")
            if bi == n // 128 - 1:
                for dc in range(2):
                    b2_ = bank()
                    for mb in range(2):
                        mm(ps[:, b2_, 0:n], vx[:, mb, (2 * hx + dc) * 128:(2 * hx + dc + 1) * 128], pTx[:, mb, 0:n],
                           mb == 0, mb == 1)
                    v_copy(u[:, 2 * hx + dc, c0:c0 + n], ps[:, b2_, 0:n], eng=("act" if dc else "dve"))
                del st_[("pT", c0, hx)]

        DEPTH = 3
        nit = len(items)
        for idx in range(nit + DEPTH):
            ia = items[idx] if idx < nit else None
            ib = items[idx - DEPTH] if 0 <= idx - DEPTH < nit else None
            if ia:
                xa1(ia)
            if ib:
                xb1(ib)
            if ia:
                xa2(ia)
            if ib:
                xb2(ib)
            if ia:
                xa3(ia)
            if ib:
                xb3(ib)
        assert not st_
        for tj in range(4):
            wt = wtile(KC * 256)
            wv = wt[:, :].rearrange("p (k n) -> p k n", k=KC)
            wdma(wv, wsrc(wxo_d, 0, KC, tj * 256, 256))
            for mmi in range(2):
                m = tj * 2 + mmi
                for (c0, n) in main_subs:
                    b = bank()
                    for kc in range(KC):
                        mm(ps[:, b, 0:n], wv[:, kc, mmi * 128:(mmi + 1) * 128], u[:, kc, c0:c0 + n], kc == 0, kc == KC - 1)
                    v_tt(h[:, m, c0:c0 + n], ps[:, b, 0:n], h[:, m, c0:c0 + n], ALU.add)

    out_ops = []

    def final_out(p_idx, next_load):
        for bi in range(NBLK):
            stg = nxt("stage", stage)
            bks = [bank(), bank()]
            sm = nxt("small", small)
            for half in range(2):
                for q in range(4):
                    kc = half * 4 + q
                    tr(ps[:, bks[half], q * 128:(q + 1) * 128], h[:, kc, bi * 128:(bi + 1) * 128], identf[:, :])
                act_fn(stg[:, half * 512:(half + 1) * 512], ps[:, bks[half], :], AF.Square, accum=sm[:, half:half + 1])
            v_tt(sm[:, 2:3], sm[:, 0:1], sm[:, 1:2], ALU.add)
            act_fn(sm[:, 3:4], sm[:, 2:3], AF.Ln, scale=1.0 / D, bias=epsc[:, :])
            act_fn(sm[:, 2:3], sm[:, 3:4], AF.Exp, scale=-0.5)
            for half in range(2):
                v_stt(stg[:, half * 512:(half + 1) * 512], ps[:, bks[half], :], sm[:, 2:3],
                      gfin[:, half * 512:(half + 1) * 512], ALU.mult, ALU.mult)
            r0 = p_idx * T + bi * 128
            o = P.dma("sp", out_d.ap()[r0:r0 + 128, :], stg[:, :], reads=[stg[:, :]])
            out_ops.append(o)
            if next_load:
                load_x(HALO + (p_idx + 1) * T + bi * 128, 1, h, col_blk0=bi)


    for p_idx in range(NPASS):
        main_subs = [(s * SUB, SUB) for s in range(NSUB)]
        halo = (p_idx == 0)
        if halo:
            load_x(0, 1, hh)
            load_x(HALO, NBLK, h)
            hook = lambda: rope_tables(0, TP, 0)
        else:
            v_copy(kA[:, 0:128], kA[:, T:T + 128])
            v_copy(kB[:, 0:128], kB[:, T:T + 128])
            v_copy(vtok[:, 0, :], vtok[:, NBLK, :])
            hook = (lambda pp: (lambda: rope_tables(128, T, HALO + pp * T)))(p_idx)
        subs = []
        if halo:
            subs.append((hh, uh, acth, HALO))
        for (c0, n) in main_subs:
            subs.append((h[:, :, c0:c0 + n], u[:, :, c0:c0 + n], act[:, :, c0:c0 + n], n))
        ffn(subs, GC_FFN1, w1i_d, w1o_d, hook=hook)
        mixer(p_idx, main_subs, halo)
        xattn(main_subs)
        subs2 = [(h[:, :, c0:c0 + n], u[:, :, c0:c0 + n], act[:, :, c0:c0 + n], n) for (c0, n) in main_subs]
        ffn(subs2, GC_FFN2, w2i_d, w2o_d)
        final_out(p_idx, p_idx + 1 < NPASS)

    P.finalize(sems)
    block = es.enter_context(nc.Block())

    @block.tensor
    def _(e):
        P.emit("pe", e)

    @block.scalar
    def _(e):
        P.emit("act", e)

    @block.vector
    def _(e):
        P.emit("dve", e)

    @block.gpsimd
    def _(e):
        P.emit("pool", e)

    @block.sync
    def _(e):
        P.emit("sp", e, final_wait=out_ops)

    es.close()
    return nc, P


def _host_consts():
    ident = np.eye(128, dtype=np.float32)
    idx = np.arange(128)
    perm32 = (idx // 64) * 64 + ((idx % 64) + 32) % 64
    p32 = np.zeros((128, 128), np.float32)
    p32[perm32, idx] = 1.0
    perm64 = (idx + 64) % 128
    p64 = np.zeros((128, 128), np.float32)
    p64[perm64, idx] = 1.0
    consts = np.ascontiguousarray(np.stack([ident, p32, p64], axis=1))
    qi = np.arange(128)[:, None]
    ki = np.arange(256)[None, :]
    rel = 128 + qi - ki
    band = (rel >= 0) & (rel < 128)
    m_std = np.where(band, 0.0, NEG).astype(np.float32)
    m_first0 = np.where(band & (ki >= 128), 0.0, NEG).astype(np.float32)
    half = 32
    inv_freq = (np.float32(10000.0) ** (-np.arange(half, dtype=np.float32) / np.float32(half))).astype(np.float32)
    return consts, m_std, m_first0, inv_freq


def make_in_maps(inputs, ntok, ncore_per_seq):
    x = np.asarray(inputs["x"], np.float32)
    mem = np.asarray(inputs["mem"], np.float32)
    pos = np.asarray(inputs["positions"], np.int32)
    B, S, _ = x.shape
    assert S == ntok * ncore_per_seq
    consts, m_std, m_first0, inv_freq = _host_consts()

    def sq(name):
        a = np.asarray(inputs[name], np.float32)
        return np.ascontiguousarray(a[0])

    def colmajor(v):
        v = np.asarray(v, np.float32)
        return v.reshape(-1, 128).T

    gains = np.zeros((128, GCOLS), np.float32)
    gains[:, GC_FFN1:GC_FFN1 + 8] = colmajor(sq("g_ffn1"))
    gains[:, GC_MIX:GC_MIX + 8] = colmajor(sq("g_mix"))
    gains[:, GC_XATT:GC_XATT + 8] = colmajor(sq("g_xattn"))
    gains[:, GC_FFN2:GC_FFN2 + 8] = colmajor(sq("g_ffn2"))
    ga = sq("g_attn_out")
    gains[:, GC_ATT:GC_ATT + 4] = ga.reshape(2, 4, 64).transpose(0, 2, 1).reshape(128, 4)
    gains[:, GC_CONVG:GC_CONVG + 4] = colmajor(sq("g_conv_out"))
    cw = sq("conv_w")
    for c in range(4):
        for tap in range(3):
            gains[:, GC_CONVW + c * 3 + tap] = cw[tap, c * 128:(c + 1) * 128]
    pidx = np.arange(128)
    gains[:, GC_INVF] = inv_freq[pidx % 32]
    gains[:, GC_SSCALE] = np.where((pidx % 64) < 32, -1.0, 1.0)
    gmem_bc = np.ascontiguousarray(np.broadcast_to(sq("g_mem")[None, :], (128, D)))
    gfin_bc = np.ascontiguousarray(np.broadcast_to(np.asarray(inputs["g_final"], np.float32)[None, :], (128, D)))
    sinks_bc = np.ascontiguousarray(np.broadcast_to(sq("sinks")[None, :], (128, 8)))
    shared = {
        "gains": gains, "gmem_bc": gmem_bc, "gfin_bc": gfin_bc, "sinks_bc": sinks_bc, "consts": consts,
        "w_ffn1_in": sq("w_ffn1_in"), "w_ffn1_out": sq("w_ffn1_out"), "w_mix_in": sq("w_mix_in"),
        "w_mix_out": sq("w_mix_out"), "w_xq": sq("w_xq"), "w_xkv": sq("w_xkv"), "w_xo": sq("w_xo"),
        "w_ffn2_in": sq("w_ffn2_in"), "w_ffn2_out": sq("w_ffn2_out"),
    }
    in_maps = []
    for b in range(B):
        for hf in range(ncore_per_seq):
            s0 = hf * ntok
            xc = np.zeros((HALO + ntok, D), np.float32)
            pc = np.zeros((HALO + ntok,), np.int32)
            xc[HALO:] = x[b, s0:s0 + ntok]
            pc[HALO:] = pos[b, s0:s0 + ntok]
            if hf > 0:
                xc[:HALO] = x[b, s0 - HALO:s0]
                pc[:HALO] = pos[b, s0 - HALO:s0]
            masks = np.ascontiguousarray(np.stack([m_first0 if hf == 0 else m_std, m_std], axis=1))
            m = dict(shared)
            m.update({"x": xc, "pos": np.ascontiguousarray(np.broadcast_to(pc[None, :], (128, HALO + ntok))),
                      "mem": np.ascontiguousarray(mem[b]), "masks": masks})
            in_maps.append(m)
    return in_maps


_NC_CACHE = {}


def kernel(**inputs):
    x = np.asarray(inputs["x"])
    B, S, _ = x.shape
    ncore = 8
    per_seq = ncore // B
    ntok = S // per_seq
    cfg = Cfg(ntok=ntok, T=1024, SUB=512)
    key = (ntok,)
    if key not in _NC_CACHE:
        _NC_CACHE[key] = build_program(cfg)[0]
    nc = _NC_CACHE[key]
    in_maps = make_in_maps(inputs, ntok, per_seq)
    res = run_bass_kernel_spmd(nc, in_maps, core_ids=list(range(ncore)))
    out = np.empty((B, S, D), np.float32)
    i = 0
    for b in range(B):
        for hf in range(per_seq):
            out[b, hf * ntok:(hf + 1) * ntok] = res.results[i]["out"]
            i += 1
    return out
```

```python
import contextlib
import numpy as np
import concourse.bass as bass
import concourse.mybir as mybir
from concourse.bass_utils import run_bass_kernel_spmd

F32 = mybir.dt.float32
BF16 = mybir.dt.bfloat16
I32 = mybir.dt.int32
AF = mybir.ActivationFunctionType
ALU = mybir.AluOpType
AX = mybir.AxisListType

D = 1024
KC = 8
DFF = 2816
FC = 22
HALO = 128
MEM = 256
RMS_EPS = 1e-5
NEG = -30000.0
PI = float(np.pi)
TWO_PI = float(2 * np.pi)

GC_FFN1, GC_MIX, GC_XATT, GC_FFN2 = 0, 8, 16, 24
GC_ATT, GC_CONVG, GC_CONVW, GC_INVF, GC_SSCALE = 32, 36, 40, 52, 53
GCOLS = 56


class Cfg:
    def __init__(self, ntok=4096, T=1024, SUB=512):
        self.ntok, self.T, self.SUB = ntok, T, SUB
        self.npass = ntok // T
        self.nsub = T // SUB
        self.nblk = T // 128
        assert ntok % T == 0 and T % SUB == 0 and SUB % 128 == 0


def dsize(dt):
    return 2 if dt == BF16 else 4


class Op:
    __slots__ = ("eng", "fn", "deps", "sem", "val", "signals", "is_dma", "seq")
    _n = 0

    def __init__(self, eng, fn):
        self.eng, self.fn = eng, fn
        Op._n += 1
        self.seq = Op._n
        self.deps = []
        self.sem = None
        self.val = None
        self.signals = False
        self.is_dma = False


class Prog:
    CELL = 256
    ENGS = ("pe", "act", "dve", "pool", "sp")

    def __init__(self, nc):
        self.nc = nc
        self.ops = {e: [] for e in self.ENGS}
        self.bufs = {}
        self.cells = {}
        self.cache = {}
        self.dma_sems = {}
        self.dma_rr = {e: 0 for e in self.ENGS}
        self.sb_next = 16512
        self.sb_end = 229376

    def sb(self, name, shape, dtype, alias=None):
        size = int(np.prod(shape[1:])) * dsize(dtype)
        if alias is None:
            addr = (self.sb_next + 255) // 256 * 256
            self.sb_next = addr + size
            assert self.sb_next <= self.sb_end, f"SBUF overflow at {name}: {self.sb_next}"
        else:
            addr = alias
            assert addr % 32 == 0
        t = self.nc.alloc_sbuf_tensor_at(name, list(shape), dtype, offset=addr)
        self.bufs[t.name] = ("sb", addr, dsize(dtype))
        return t, addr, size

    def reg_psum(self, t):
        self.bufs[t.name] = ("ps", 0, 4)

    def _cells(self, ap):
        key = (ap.tensor.name, ap.offset, ap.ap)
        r = self.cache.get(key)
        if r is not None:
            return r
        space, base, ds = self.bufs[ap.tensor.name]
        dims = ap.ap
        pstep = dims[0][0]
        fo = ap.offset % pstep if pstep > 0 else ap.offset
        free = dims[1:]
        if not free:
            free = ((1, 1),)
        last_step, last_n = free[-1]
        starts = [fo]
        for (st, n) in free[:-1]:
            starts = [s + i * st for s in starts for i in range(n)]
        span = (last_n - 1) * abs(last_step) + 1
        csz = 2048 if space == "ps" else self.CELL
        cs = set()
        for s in starts:
            lo = base + s * ds
            hi = base + (s + span) * ds
            for c in range(lo // csz, (hi - 1) // csz + 1):
                cs.add((space, c))
        r = tuple(cs)
        self.cache[key] = r
        return r

    def _track(self, op, reads, writes):
        key = id(op) if op.is_dma else op.eng
        cands = {}

        def cand(other):
            if other is None or other is op:
                return
            if other.is_dma:
                cands[id(other)] = other
                return
            if other.eng == "pe" and op.eng == "pe" and not op.is_dma:
                return
            cur = cands.get(other.eng)
            if cur is None or cur.seq < other.seq:
                cands[other.eng] = other

        for ap in reads:
            for c in self._cells(ap):
                st = self.cells.get(c)
                if st is None:
                    st = [None, {}]
                    self.cells[c] = st
                cand(st[0])
                if c[0] == "ps":
                    for r in st[1].values():
                        if r.eng != op.eng:
                            cand(r)
                st[1][key] = op
        for ap in writes:
            for c in self._cells(ap):
                st = self.cells.get(c)
                if st is None:
                    st = [None, {}]
                    self.cells[c] = st
                cand(st[0])
                for r in st[1].values():
                    cand(r)
                st[0] = op
                st[1] = {}
        for other in cands.values():
            op.deps.append(other)
            other.signals = True

    def op(self, eng, fn, reads=(), writes=()):
        o = Op(eng, fn)
        self._track(o, reads, writes)
        self.ops[eng].append(o)
        return o

    def dma(self, eng, out, in_, reads=(), writes=()):
        o = Op(eng, None)
        o.is_dma = True
        pool = self.dma_sems[eng]
        slot = pool[self.dma_rr[eng] % len(pool)]
        self.dma_rr[eng] += 1
        if slot[2] is not None:
            o.deps.append(slot[2])
        slot[1] += 16
        slot[2] = o
        o.sem, o.val = slot[0], slot[1]
        o.signals = True
        o.fn = (out, in_)
        self._track(o, reads, writes)
        self.ops[eng].append(o)
        return o

    def finalize(self, sems):
        for e in self.ENGS:
            n = 0
            for o in self.ops[e]:
                if o.is_dma:
                    continue
                o.sem = sems[e]
                if o.signals:
                    n += 1
                    o.val = n

    def emit(self, eng, h, final_wait=()):
        seen = {}
        for o in self.ops[eng]:
            need = {}
            for d in o.deps:
                assert d.val is not None
                k = d.sem
                if seen.get(k.num, 0) >= d.val:
                    continue
                if need.get(k.num, (None, 0))[1] < d.val:
                    need[k.num] = (k, d.val)
            for num, (k, v) in need.items():
                h.wait_ge(k, v)
                seen[num] = v
            if o.is_dma:
                out, in_ = o.fn
                h.dma_start(out=out, in_=in_).then_inc(o.sem, 16)
            else:
                ins = o.fn(h)
                if o.signals:
                    ins.then_inc(o.sem, 1)
        for d in final_wait:
            if seen.get(d.sem.num, 0) < d.val:
                h.wait_ge(d.sem, d.val)
                seen[d.sem.num] = d.val


def build_program(cfg):
    nc = bass.Bass("TRN2", target_bir_lowering=False)
    P = Prog(nc)
    es = contextlib.ExitStack()
    sems = {e: es.enter_context(nc.semaphore(f's_{e}')) for e in Prog.ENGS}
    P.dma_sems = {'pool': [[es.enter_context(nc.semaphore(f'dp{i}')), 0, None] for i in range(16)],
                  'sp': [[es.enter_context(nc.semaphore(f'ds{i}')), 0, None] for i in range(8)]}
    T, SUB, NSUB, NBLK, NPASS = cfg.T, cfg.SUB, cfg.nsub, cfg.nblk, cfg.npass
    NTOT = HALO + cfg.ntok
    TP = T + 128

    def din(name, shape, dt=F32):
        return nc.dram_tensor(name, list(shape), dt, kind="ExternalInput")

    x_d = din("x", [NTOT, D])
    pos_d = din("pos", [128, NTOT], I32)
    mem_d = din("mem", [MEM, D])
    gains_d = din("gains", [128, GCOLS])
    gmem_d = din("gmem_bc", [128, D])
    gfin_d = din("gfin_bc", [128, D])
    sinks_d = din("sinks_bc", [128, 8])
    consts_d = din("consts", [128, 3, 128])
    masks_d = din("masks", [128, 2, 256])
    w1i_d = din("w_ffn1_in", [D, 2 * DFF])
    w1o_d = din("w_ffn1_out", [DFF, D])
    wmi_d = din("w_mix_in", [D, 2304])
    wmo_d = din("w_mix_out", [D, D])
    wxq_d = din("w_xq", [D, D])
    wxkv_d = din("w_xkv", [D, 2 * D])
    wxo_d = din("w_xo", [D, D])
    w2i_d = din("w_ffn2_in", [D, 2 * DFF])
    w2o_d = din("w_ffn2_out", [DFF, D])
    out_d = nc.dram_tensor("out", [cfg.ntok, D], F32, kind="ExternalOutput")

    h, _, _ = P.sb("h", [128, KC, T], F32)
    u, _, _ = P.sb("u", [128, KC, T], BF16)
    bufA, bufA_addr, bufA_size = P.sb("bufA", [128, KC, max(T, 1024)], BF16)
    hh, _, _ = P.sb("hh", [128, KC, HALO], F32, alias=bufA_addr)
    uh, _, _ = P.sb("uh", [128, KC, HALO], BF16, alias=bufA_addr + 4096)
    acth, _, _ = P.sb("acth", [128, FC, HALO], BF16, alias=bufA_addr + 4096 + 2048)
    assert 4096 + 2048 + FC * HALO * 2 <= bufA_size
    kA, _, _ = P.sb("kA", [128, TP], BF16)
    kB, _, _ = P.sb("kB", [128, TP], BF16)
    vtok, _, _ = P.sb("vtok", [128, NBLK + 1, 128], BF16)
    zprev, _, _ = P.sb("zprev", [128, 4, 2], F32)
    kx, _, _ = P.sb("kx", [128, KC, MEM], BF16)
    vx, _, _ = P.sb("vx", [128, 2, D], BF16)
    gains, _, _ = P.sb("gains", [128, GCOLS], F32)
    sink8, _, _ = P.sb("sink8", [128, 8], F32)
    identf, _, _ = P.sb("identf", [128, 128], F32)
    cbf, _, _ = P.sb("cbf", [128, 4, 128], BF16)
    masks, _, _ = P.sb("masks", [128, 2, 256], F32)
    epsc, _, _ = P.sb("epsc", [128, 1], F32)
    maskb, _, _ = P.sb("maskb", [128, 2, 256], BF16)
    sinkb, _, _ = P.sb("sinkb", [128, 8], BF16)
    gfin, _, _ = P.sb("gfin", [128, D], F32)
    stage = [P.sb(f"stage{i}", [128, D], F32)[0] for i in range(2)]
    xstage = [P.sb(f"xstage{i}", [128, D], F32)[0] for i in range(2)]
    rstd = [P.sb(f"rstd{i}", [128, SUB], F32)[0] for i in range(2)]
    lnv = [P.sb(f"lnv{i}", [128, SUB], F32)[0] for i in range(2)]
    small = [P.sb(f"small{i}", [128, 4], F32)[0] for i in range(8)]
    sgb = [P.sb(f"sgb{i}", [128, SUB], F32)[0] for i in range(2)]
    tmpb = [P.sb(f"tmpb{i}", [128, max(SUB, 512)], BF16)[0] for i in range(2)]
    act, scr_addr, act_size = P.sb("act", [128, FC, T], BF16)
    a = [scr_addr]

    def scr(name, shape, dt):
        t_, _, sz = P.sb(name, shape, dt, alias=a[0])
        a[0] += (sz + 255) // 256 * 256
        return t_

    qr = scr("qr", [128, 4, T], BF16)
    pT = [scr(f"pT{i}", [128, 2, 512], BF16) for i in range(2)]
    zb2 = [scr(f"zb{i}", [128, SUB + 2], F32) for i in range(2)]
    yb_addr = a[0]
    ybs = [scr(f"yb{i}", [128, 4, SUB], F32) for i in range(NSUB)]
    if 4 * 1280 + 4 * 512 <= NSUB * 4 * SUB * 4:
        pf = [P.sb(f"pf{i}", [128, 264], F32, alias=yb_addr + i * 1280)[0] for i in range(4)]
        pb = [P.sb(f"pb{i}", [128, 256], BF16, alias=yb_addr + 4 * 1280 + i * 512)[0] for i in range(4)]
    else:
        pf = [scr(f"pf{i}", [128, 264], F32) for i in range(4)]
        pb = [scr(f"pb{i}", [128, 256], BF16) for i in range(4)]
    ssb = scr("ssb", [128, 8, 264], F32)
    assert a[0] <= scr_addr + act_size or T < 1024, (a[0] - scr_addr, act_size)
    a[0] = max(a[0], scr_addr + act_size)
    cos2 = scr("cos2", [128, TP], F32)
    sinS = scr("sinS", [128, TP], F32)
    tmpf_addr = a[0]
    tmpf = [scr(f"tmpf{i}", [128, SUB], F32) for i in range(3)]
    scr_end = a[0]
    RCH = ((3 * SUB) // 256 * 256) // 4
    a[0] = tmpf_addr
    posi = scr("posi", [128, RCH], I32)
    posf = scr("posf", [128, RCH], F32)
    ang = scr("ang", [128, RCH], F32)
    kf = scr("kf", [128, RCH], F32)
    assert a[0] <= scr_end
    P.sb_next = max(P.sb_next, scr_end)
    memt = [P.sb(f"memt{i}", [128, D], F32, alias=scr_addr + i * 4096)[0] for i in range(2)]
    memn = [P.sb(f"memn{i}", [128, D], F32, alias=scr_addr + 8192 + i * 4096)[0] for i in range(2)]
    memT, _, _ = P.sb("memT", [128, KC, MEM], BF16, alias=scr_addr + 16384)
    cstage, _, _ = P.sb("cstage", [128, 3, 128], F32, alias=scr_addr + 16384 + 4096)
    ring_addr = (P.sb_next + 255) // 256 * 256
    ring_size = (P.sb_end - ring_addr) // 256 * 256
    assert ring_size >= 24 * 1024, f"ring too small {ring_size}"
    ring_pos = [0]
    wcount = [0]

    def wtile(nfree):
        size = (nfree * 2 + 255) // 256 * 256
        if ring_pos[0] + size > ring_size:
            ring_pos[0] = 0
        addr = ring_addr + ring_pos[0]
        ring_pos[0] += size
        wcount[0] += 1
        t_ = nc.alloc_sbuf_tensor_at(f"wt{wcount[0]}", [128, nfree], BF16, offset=addr)
        P.bufs[t_.name] = ("sb", addr, 2)
        return t_

    ps = nc.alloc_psum_tensor("ps", [128, 8, 512], F32)
    P.reg_psum(ps)
    bank_rr = [0]

    def bank():
        b = bank_rr[0] % 6
        bank_rr[0] += 1
        return b

    def wdma(dst, src):
        return P.dma("pool", dst, src, writes=[dst])

    def ldma(dst, src):
        return P.dma("sp", dst, src, writes=[dst])

    def mm(out, lhsT, rhs, start, stop):
        return P.op("pe", lambda e: e.matmul(out, lhsT=lhsT, rhs=rhs, start=start, stop=stop),
                    reads=[lhsT, rhs], writes=[out])

    def tr(out, in_, ident):
        return P.op("pe", lambda e: e.transpose(out, in_, ident), reads=[in_, ident], writes=[out])

    def act_fn(out, in_, func, scale=1.0, bias=None, accum=None):
        reads = [in_]
        kw = {}
        if bias is not None:
            kw["bias"] = bias
            reads.append(bias)
        if not isinstance(scale, float):
            reads.append(scale)
        writes = [out]
        if accum is not None:
            kw["accum_out"] = accum
            writes.append(accum)
        return P.op("act", lambda e: e.activation(out=out, in_=in_, func=func, scale=scale, **kw),
                    reads=reads, writes=writes)

    def v_tt(out, in0, in1, op):
        return P.op("dve", lambda e: e.tensor_tensor(out=out, in0=in0, in1=in1, op=op),
                    reads=[in0, in1], writes=[out])

    def v_ts(out, in0, s1, op0, s2=None, op1=None):
        reads = [in0] + [s for s in (s1, s2) if s is not None and not isinstance(s, float)]
        if op1 is None:
            return P.op("dve", lambda e: e.tensor_scalar(out=out, in0=in0, scalar1=s1, scalar2=None, op0=op0),
                        reads=reads, writes=[out])
        return P.op("dve", lambda e: e.tensor_scalar(out=out, in0=in0, scalar1=s1, scalar2=s2, op0=op0, op1=op1),
                    reads=reads, writes=[out])

    def v_stt(out, in0, scalar, in1, op0, op1):
        reads = [in0, in1] + ([] if isinstance(scalar, float) else [scalar])
        return P.op("dve", lambda e: e.scalar_tensor_tensor(out=out, in0=in0, scalar=scalar, in1=in1, op0=op0, op1=op1),
                    reads=reads, writes=[out])

    def v_copy(out, in_, eng="dve"):
        if eng == "act":
            return P.op("act", lambda e: e.copy(out=out, in_=in_), reads=[in_], writes=[out])
        return P.op("dve", lambda e: e.tensor_copy(out=out, in_=in_), reads=[in_], writes=[out])

    def v_max(out, in_):
        return P.op("dve", lambda e: e.tensor_reduce(out=out, in_=in_, axis=AX.X, op=ALU.max),
                    reads=[in_], writes=[out])

    def v_recip(out, in_):
        return P.op("dve", lambda e: e.reciprocal(out=out, in_=in_), reads=[in_], writes=[out])

    def v_memset(ap, val):
        return P.op("dve", lambda e: e.memset(ap, val), writes=[ap])

    def wsrc(wd, r0, nr, c0, ncol):
        return wd.ap()[r0:r0 + nr * 128, c0:c0 + ncol].rearrange("(kc p) n -> p kc n", p=128)

    ones = cbf[:, 0, :]
    p32 = cbf[:, 1, :]
    p64 = cbf[:, 2, :]
    identb = cbf[:, 3, :]
    rr = {}

    def nxt(key, lst):
        i = rr.get(key, 0)
        rr[key] = i + 1
        return lst[i % len(lst)]

    ldma(gains[:, :], gains_d.ap())
    ldma(cstage[:, :, :], consts_d.ap())
    ldma(masks[:, :, :], masks_d.ap())
    ldma(gfin[:, :], gfin_d.ap())
    v_memset(epsc[:, :], RMS_EPS)
    v_memset(cbf[:, 0, :], 1.0)
    v_copy(identf[:, :], cstage[:, 0, :])
    v_copy(cbf[:, 1, :], cstage[:, 1, :])
    v_copy(cbf[:, 2, :], cstage[:, 2, :])
    v_copy(cbf[:, 3, :], cstage[:, 0, :])
    v_memset(zprev[:, :, :], 0.0)
    v_copy(maskb[:, :, :], masks[:, :, :])
    P.dma("pool", sinkb[:, :], sinks_d.ap(), writes=[sinkb[:, :]])

    def rms_stats_fm(src_chunks, n, nfeat):
        b = bank()
        ssp = ps[:, b, 0:n]
        nchunk = len(src_chunks)
        for i, s in enumerate(src_chunks):
            sq = nxt("tmpb", tmpb)[:, 0:n]
            act_fn(sq, s, AF.Square)
            mm(ssp, ones, sq, i == 0, i == nchunk - 1)
        i_ = rr.get("rstd", 0)
        rr["rstd"] = i_ + 1
        lv = lnv[i_ % 2][:, 0:n]
        rs = rstd[i_ % 2][:, 0:n]
        act_fn(lv, ssp, AF.Ln, scale=1.0 / nfeat, bias=epsc[:, :])
        act_fn(rs, lv, AF.Exp, scale=-0.5)
        return rs

    def rmsnorm_to_u(hsrc, udst, n, gcol):
        rs = rms_stats_fm([hsrc[:, kc, :] for kc in range(KC)], n, D)
        for kc in range(KC):
            v_stt(udst[:, kc, :], hsrc[:, kc, :], gains[:, gcol + kc:gcol + kc + 1], rs, ALU.mult, ALU.mult)

    ldma(memt[0][:, :], mem_d.ap()[0:128, :])
    ldma(memt[1][:, :], mem_d.ap()[128:256, :])
    gmem_sb = stage[0]
    ldma(gmem_sb[:, :], gmem_d.ap())
    for mb in range(2):
        ssq = small[0][:, mb:mb + 1]
        act_fn(stage[1][:, :], memt[mb][:, :], AF.Square, accum=ssq)
        lv_ = small[1][:, mb:mb + 1]
        act_fn(lv_, ssq, AF.Ln, scale=1.0 / D, bias=epsc[:, :])
        rs_ = small[2][:, mb:mb + 1]
        act_fn(rs_, lv_, AF.Exp, scale=-0.5)
        v_stt(memn[mb][:, :], memt[mb][:, :], rs_, gmem_sb[:, :], ALU.mult, ALU.mult)
        for half in range(2):
            b = bank()
            for q in range(4):
                kc = half * 4 + q
                tr(ps[:, b, q * 128:(q + 1) * 128], memn[mb][:, kc * 128:(kc + 1) * 128], identf[:, :])
            v_copy(memT[:, half * 4:half * 4 + 4, mb * 128:(mb + 1) * 128],
                   ps[:, b, :].rearrange("p (a c) -> p a c", a=4), eng=("act" if half else "dve"))
    for ti in range(4):
        wt = wtile(KC * 256)
        wv = wt[:, :].rearrange("p (k n) -> p k n", k=KC)
        wdma(wv, wsrc(wxkv_d, 0, KC, ti * 256, 256))
        for cc in range(2):
            oc = ti * 2 + cc
            b = bank()
            for kc in range(KC):
                mm(ps[:, b, 0:MEM], wv[:, kc, cc * 128:(cc + 1) * 128], memT[:, kc, :], kc == 0, kc == KC - 1)
            v_copy(kx[:, oc, :], ps[:, b, 0:MEM], eng=("act" if cc else "dve"))
    for ti in range(2):
        wt = wtile(KC * 512)
        wv = wt[:, :].rearrange("p (k n) -> p k n", k=KC)
        wdma(wv, wsrc(wxkv_d, 0, KC, D + ti * 512, 512))
        for mb in range(2):
            b = bank()
            for kc in range(KC):
                mm(ps[:, b, :], memT[:, kc, mb * 128:(mb + 1) * 128], wv[:, kc, :], kc == 0, kc == KC - 1)
            v_copy(vx[:, mb, ti * 512:(ti + 1) * 512], ps[:, b, :], eng=("act" if mb else "dve"))

    def x_dma(row0):
        stg = nxt("xstage", xstage)
        ldma(stg[:, :], x_d.ap()[row0:row0 + 128, :])
        return stg

    def x_tr(stg, hdst, bi):
        for half in range(2):
            b = bank()
            for q in range(4):
                kc = half * 4 + q
                tr(ps[:, b, q * 128:(q + 1) * 128], stg[:, kc * 128:(kc + 1) * 128], identf[:, :])
            v_copy(hdst[:, half * 4:half * 4 + 4, bi * 128:(bi + 1) * 128],
                   ps[:, b, :].rearrange("p (a c) -> p a c", a=4), eng=("act" if half else "dve"))

    def load_x(row0, nblocks, hdst):
        for bi in range(nblocks):
            x_tr(x_dma(row0 + bi * 128), hdst, bi)

    def ffn(subs, gcol, wi_d, wo_d, hook=None):
        for (hs, us, as_, n) in subs:
            rmsnorm_to_u(hs, us, n, gcol)
        if hook is not None:
            hook()
        for ti in range(FC // 2):
            wt = wtile(KC * 512)
            wv = wt[:, :].rearrange("p (k n) -> p k n", k=KC)
            wdma(wv[:, :, 0:256], wsrc(wi_d, 0, KC, ti * 256, 256))
            wdma(wv[:, :, 256:512], wsrc(wi_d, 0, KC, DFF + ti * 256, 256))
            for cc in range(2):
                c = ti * 2 + cc
                for (hs, us, as_, n) in subs:
                    bg, bu = bank(), bank()
                    for kc in range(KC):
                        mm(ps[:, bg, 0:n], wv[:, kc, cc * 128:(cc + 1) * 128], us[:, kc, :], kc == 0, kc == KC - 1)
                    for kc in range(KC):
                        mm(ps[:, bu, 0:n], wv[:, kc, 256 + cc * 128:256 + (cc + 1) * 128], us[:, kc, :], kc == 0, kc == KC - 1)
                    sg = nxt("sgb", sgb)[:, 0:n]
                    act_fn(sg, ps[:, bg, 0:n], AF.Silu)
                    v_tt(as_[:, c, :], sg, ps[:, bu, 0:n], ALU.mult)
        for tj in range(4):
            wt = wtile(FC * 256)
            wv = wt[:, :].rearrange("p (k n) -> p k n", k=FC)
            wdma(wv[:, 0:11, :], wsrc(wo_d, 0, 11, tj * 256, 256))
            wdma(wv[:, 11:22, :], wsrc(wo_d, 11 * 128, 11, tj * 256, 256))
            for mmi in range(2):
                m = tj * 2 + mmi
                for (hs, us, as_, n) in subs:
                    b = bank()
                    for kc in range(FC):
                        mm(ps[:, b, 0:n], wv[:, kc, mmi * 128:(mmi + 1) * 128], as_[:, kc, :], kc == 0, kc == FC - 1)
                    v_stt(hs[:, m, :], ps[:, b, 0:n], 0.5, hs[:, m, :], ALU.mult, ALU.add)

    def rope_tables(col0, ncols, pos_col0):
        for o in range(0, ncols, RCH):
            w = min(RCH, ncols - o)
            sl = slice(col0 + o, col0 + o + w)
            tl = slice(0, w)
            ldma(posi[:, tl], pos_d.ap()[:, pos_col0 + o:pos_col0 + o + w])
            v_copy(posf[:, tl], posi[:, tl])
            v_ts(ang[:, tl], posf[:, tl], gains[:, GC_INVF:GC_INVF + 1], ALU.mult)
            for which in range(2):
                dst = sinS if which == 0 else cos2
                src_ = ang
                if which == 1:
                    v_ts(posf[:, tl], ang[:, tl], PI / 2, ALU.add)
                    src_ = posf
                v_ts(kf[:, tl], src_[:, tl], 1.0 / TWO_PI, ALU.mult)
                v_copy(posi[:, tl], kf[:, tl])
                v_copy(kf[:, tl], posi[:, tl])
                C1 = 6.28125
                C2 = TWO_PI - C1
                v_stt(dst[:, sl], kf[:, tl], -C1, src_[:, tl], ALU.mult, ALU.add)
                v_stt(dst[:, sl], kf[:, tl], -C2, dst[:, sl], ALU.mult, ALU.add)
                v_ts(kf[:, tl], dst[:, sl], PI, ALU.is_gt)
                v_stt(dst[:, sl], kf[:, tl], -TWO_PI, dst[:, sl], ALU.mult, ALU.add)
                v_ts(kf[:, tl], dst[:, sl], -PI, ALU.is_lt)
                v_stt(dst[:, sl], kf[:, tl], TWO_PI, dst[:, sl], ALU.mult, ALU.add)
                v_ts(dst[:, sl], dst[:, sl], PI, ALU.min, -PI, ALU.max)
                if which == 0:
                    act_fn(dst[:, sl], dst[:, sl], AF.Sin, scale=gains[:, GC_SSCALE:GC_SSCALE + 1])
                else:
                    act_fn(dst[:, sl], dst[:, sl], AF.Sin)

    def rope_apply(psrc_bank, n, tcol0, dst_ap, pre=None):
        qb = nxt("tmpb", tmpb)[:, 0:n]
        v_copy(qb, ps[:, psrc_bank, 0:n], eng="act")
        b2 = bank()
        mm(ps[:, b2, 0:n], p32, qb, True, True)
        t1 = nxt("tmpf", tmpf)[:, 0:n]
        t2 = nxt("tmpf", tmpf)[:, 0:n]
        if pre is None:
            v_tt(t1, ps[:, psrc_bank, 0:n], cos2[:, tcol0:tcol0 + n], ALU.mult)
            v_tt(t2, ps[:, b2, 0:n], sinS[:, tcol0:tcol0 + n], ALU.mult)
        else:
            v_stt(t1, ps[:, psrc_bank, 0:n], pre, cos2[:, tcol0:tcol0 + n], ALU.mult, ALU.mult)
            v_stt(t2, ps[:, b2, 0:n], pre, sinS[:, tcol0:tcol0 + n], ALU.mult, ALU.mult)
        v_tt(dst_ap, t1, t2, ALU.add)

    def mixer(p_idx, main_subs, halo):
        if halo:
            rmsnorm_to_u(hh, uh, HALO, GC_MIX)
        for (c0, n) in main_subs:
            rmsnorm_to_u(h[:, :, c0:c0 + n], u[:, :, c0:c0 + n], n, GC_MIX)
        toks = []
        if halo:
            toks.append((uh, HALO, 0, True, None))
        for (c0, n) in main_subs:
            toks.append((u[:, :, c0:c0 + n], n, 128 + c0, False, c0))
        for ti in range(2):
            wt = wtile(KC * 256)
            wv = wt[:, :].rearrange("p (k n) -> p k n", k=KC)
            wdma(wv, wsrc(wmi_d, 0, KC, ti * 256, 256))
            for cc in range(2):
                c = ti * 2 + cc
                for (us, n, tc0, ish, c0) in toks:
                    if ish:
                        continue
                    b = bank()
                    for kc in range(KC):
                        mm(ps[:, b, 0:n], wv[:, kc, cc * 128:(cc + 1) * 128], us[:, kc, :], kc == 0, kc == KC - 1)
                    rope_apply(b, n, tc0, qr[:, c, c0:c0 + n], pre=0.125)
        wt = wtile(KC * 256)
        wv = wt[:, :].rearrange("p (k n) -> p k n", k=KC)
        wdma(wv, wsrc(wmi_d, 0, KC, 512, 256))
        for (us, n, tc0, ish, c0) in toks:
            b = bank()
            for kc in range(KC):
                mm(ps[:, b, 0:n], wv[:, kc, 0:128], us[:, kc, :], kc == 0, kc == KC - 1)
            rope_apply(b, n, tc0, kA[:, tc0:tc0 + n])
            b3 = bank()
            mm(ps[:, b3, 0:n], p64, kA[:, tc0:tc0 + n], True, True)
            v_copy(kB[:, tc0:tc0 + n], ps[:, b3, 0:n], eng="act")
            b4 = bank()
            nb_ = n // 128
            for bi in range(nb_):
                for kc in range(KC):
                    mm(ps[:, b4, bi * 128:(bi + 1) * 128], us[:, kc, bi * 128:(bi + 1) * 128], wv[:, kc, 128:256],
                       kc == 0, kc == KC - 1)
            blk0 = tc0 // 128
            v_copy(vtok[:, blk0:blk0 + nb_, :], ps[:, b4, 0:n].rearrange("p (a c) -> p a c", a=nb_))
        for c in range(4):
            wt = wtile(KC * 384)
            wv = wt[:, :].rearrange("p (k n) -> p k n", k=KC)
            wdma(wv[:, :, 0:128], wsrc(wmi_d, 0, KC, 768 + c * 128, 128))
            wdma(wv[:, :, 128:256], wsrc(wmi_d, 0, KC, 1280 + c * 128, 128))
            wdma(wv[:, :, 256:384], wsrc(wmi_d, 0, KC, 1792 + c * 128, 128))
            for (us, n, tc0, ish, c0) in toks:
                bgc, bxc = bank(), bank()
                for kc in range(KC):
                    mm(ps[:, bgc, 0:n], wv[:, kc, 128:256], us[:, kc, :], kc == 0, kc == KC - 1)
                for kc in range(KC):
                    mm(ps[:, bxc, 0:n], wv[:, kc, 256:384], us[:, kc, :], kc == 0, kc == KC - 1)
                xs = nxt("tmpf", tmpf)[:, 0:n]
                v_copy(xs, ps[:, bxc, 0:n], eng="act")
                if ish:
                    zt = nxt("tmpf", tmpf)[:, 0:n]
                    v_tt(zt, ps[:, bgc, 0:n], xs, ALU.mult)
                    v_copy(zprev[:, c, :], zt[:, n - 2:n])
                    continue
                bgb = bank()
                for kc in range(KC):
                    mm(ps[:, bgb, 0:n], wv[:, kc, 0:128], us[:, kc, :], kc == 0, kc == KC - 1)
                sidx = c0 // SUB
                zz = nxt("zb", zb2)
                v_copy(zz[:, 0:2], zprev[:, c, :])
                v_tt(zz[:, 2:2 + n], ps[:, bgc, 0:n], xs, ALU.mult)
                v_copy(zprev[:, c, :], zz[:, n:n + 2])
                cv = nxt("tmpf", tmpf)[:, 0:n]
                gw = GC_CONVW + c * 3
                v_ts(cv, zz[:, 2:2 + n], gains[:, gw + 2:gw + 3], ALU.mult)
                v_stt(cv, zz[:, 1:1 + n], gains[:, gw + 1:gw + 2], cv, ALU.mult, ALU.add)
                v_stt(cv, zz[:, 0:n], gains[:, gw:gw + 1], cv, ALU.mult, ALU.add)
                v_tt(ybs[sidx][:, c, 0:n], cv, ps[:, bgb, 0:n], ALU.mult)
                if c == 3:
                    rs = rms_stats_fm([ybs[sidx][:, cc_, 0:n] for cc_ in range(4)], n, 512)
                    for cc_ in range(4):
                        v_stt(bufA[:, 4 + cc_, c0:c0 + n], ybs[sidx][:, cc_, 0:n],
                              gains[:, GC_CONVG + cc_:GC_CONVG + cc_ + 1], rs, ALU.mult, ALU.mult)
        items = [(bi, g, j) for bi in range(NBLK) for g in range(2) for j in range(4)]
        st_ = {}

        def a0(it):
            bi, g, j = it
            hd = 4 * g + j
            c = hd // 2
            half = hd % 2
            pl = slice(half * 64, half * 64 + 64)
            kk = kA if half == g else kB
            gblk = p_idx * NBLK + bi
            mk = maskb[:, 0, :] if gblk == 0 else maskb[:, 1, :]
            b = bank()
            mm(ps[:, b, 0:256], qr[pl, c, bi * 128:(bi + 1) * 128], kk[pl, bi * 128:bi * 128 + 256], True, False)
            mm(ps[:, b, 0:256], identb, mk, False, False)
            mm(ps[:, b, 256:257], identb, sinkb[:, hd:hd + 1], False, True)
            st_[("S", it)] = b

        def a1(it):
            b = st_.pop(("S", it))
            sm = nxt("small", small)
            st_[it] = [sm, nxt("pf", pf), nxt("pb", pb), b]
            P.op("dve", lambda e: e.tensor_reduce(out=sm[:, 0:1], in_=ps[:, b, 0:257], axis=AX.X, op=ALU.max, negate=True),
                 reads=[ps[:, b, 0:257]], writes=[sm[:, 0:1]])

        def a2(it):
            pass

        def a3(it):
            sm, pfb, pbb, b = st_[it]
            act_fn(pfb[:, 0:257], ps[:, b, 0:257], AF.Exp, bias=sm[:, 0:1], accum=sm[:, 2:3])
            st_[it] = [sm, pfb, pbb]

        def b1(it):
            sm, pfb, pbb = st_[it]
            v_recip(sm[:, 3:4], sm[:, 2:3])

        def b2(it):
            sm, pfb, pbb = st_[it]
            v_ts(pbb[:, 0:256], pfb[:, 0:256], sm[:, 3:4], ALU.mult)

        def b3(it):
            bi, g, j = it
            sm, pfb, pbb = st_.pop(it)
            if j == 0:
                st_[("pT", bi, g)] = nxt("pT", pT)
            pTg = st_[("pT", bi, g)]
            bt = bank()
            for kb in range(2):
                mm(ps[:, bt, kb * 128:(kb + 1) * 128], pbb[:, kb * 128:(kb + 1) * 128], identb, True, True)
            v_copy(pTg[:, :, j * 128:(j + 1) * 128], ps[:, bt, 0:256].rearrange("p (a c) -> p a c", a=2), eng="act")
            gblk = p_idx * NBLK + bi
            bo = 6 + (gblk % 2)
            if j == 3:
                for kb in range(2):
                    mm(ps[g * 64:(g + 1) * 64, bo, :], vtok[:, bi + kb, g * 64:(g + 1) * 64], pTg[:, kb, :],
                       kb == 0, kb == 1)
                del st_[("pT", bi, g)]
            if j == 3 and g == 1:
                qcols = slice(bi * 128, (bi + 1) * 128)
                sqa = nxt("tmpb", tmpb)
                act_fn(sqa[:, 0:512], ps[:, bo, :], AF.Square)
                bs_ = bank()
                for jj in range(4):
                    mm(ps[:, bs_, 0:128], ones, sqa[:, jj * 128:(jj + 1) * 128], jj == 0, jj == 3)
                i_ = rr.get("rstd", 0)
                rr["rstd"] = i_ + 1
                lv = lnv[i_ % 2][:, 0:128]
                rs = rstd[i_ % 2][:, 0:128]
                act_fn(lv, ps[:, bs_, 0:128], AF.Ln, scale=1.0 / 512, bias=epsc[:, :])
                act_fn(rs, lv, AF.Exp, scale=-0.5)
                st_[("nrm", bi)] = (bo, rs, qcols)

        def b4(bi):
            bo, rs, qcols = st_.pop(("nrm", bi))
            for jj in range(4):
                v_stt(bufA[:, jj, qcols], ps[:, bo, jj * 128:(jj + 1) * 128],
                      gains[:, GC_ATT + jj:GC_ATT + jj + 1], rs, ALU.mult, ALU.mult)

        DEPTH = 3
        nit = len(items)
        a0(items[0])
        for idx in range(nit + DEPTH + 1):
            ia = items[idx] if idx < nit else None
            ib = items[idx - DEPTH] if 0 <= idx - DEPTH < nit else None
            if idx + 1 < nit:
                a0(items[idx + 1])
            if ia:
                a1(ia)
            if ib:
                b1(ib)
            if ia:
                a2(ia)
            if ib:
                b2(ib)
            if ia:
                a3(ia)
            if ib:
                b3(ib)
            il = idx - DEPTH - 1
            if 0 <= il < nit and items[il][1] == 1 and items[il][2] == 3:
                b4(items[il][0])
        assert not st_, st_.keys()
        for tj in range(4):
            wt = wtile(KC * 256)
            wv = wt[:, :].rearrange("p (k n) -> p k n", k=KC)
            for g in range(2):
                src = wmo_d.ap()[g * 256:(g + 1) * 256, tj * 256:(tj + 1) * 256].rearrange("(j d) n -> d j n", d=64)
                wdma(wv[g * 64:(g + 1) * 64, 0:4, :], src)
            wdma(wv[:, 4:8, :], wsrc(wmo_d, 512, 4, tj * 256, 256))
            for mmi in range(2):
                m = tj * 2 + mmi
                for (c0, n) in main_subs:
                    b = bank()
                    for kc in range(KC):
                        mm(ps[:, b, 0:n], wv[:, kc, mmi * 128:(mmi + 1) * 128], bufA[:, kc, c0:c0 + n], kc == 0, kc == KC - 1)
                    v_tt(h[:, m, c0:c0 + n], ps[:, b, 0:n], h[:, m, c0:c0 + n], ALU.add)

    def xattn(main_subs):
        for (c0, n) in main_subs:
            rmsnorm_to_u(h[:, :, c0:c0 + n], u[:, :, c0:c0 + n], n, GC_XATT)
        for tj in range(4):
            wt = wtile(KC * 256)
            wv = wt[:, :].rearrange("p (k n) -> p k n", k=KC)
            wdma(wv, wsrc(wxq_d, 0, KC, tj * 256, 256))
            for mmi in range(2):
                m = tj * 2 + mmi
                for (c0, n) in main_subs:
                    b = bank()
                    for kc in range(KC):
                        mm(ps[:, b, 0:n], wv[:, kc, mmi * 128:(mmi + 1) * 128], u[:, kc, c0:c0 + n], kc == 0, kc == KC - 1)
                    v_copy(bufA[:, m, c0:c0 + n], ps[:, b, 0:n], eng=("act" if mmi else "dve"))
        items = [(c0, n, hx, bi) for (c0, n) in main_subs for hx in range(4) for bi in range(n // 128)]
        st_ = {}

        def xa0(it):
            c0, n, hx, bi = it
            qc = slice(c0 + bi * 128, c0 + (bi + 1) * 128)
            b = bank()
            for dc in range(2):
                mm(ps[:, b, 0:MEM], bufA[:, 2 * hx + dc, qc], kx[:, 2 * hx + dc, :], dc == 0, dc == 1)
            st_[("S", it)] = b

        def xa1(it):
            b = st_.pop(("S", it))
            sm = nxt("small", small)
            pfb = nxt("pf", pf)
            pbb = nxt("pb", pb)
            st_[it] = (sm, pfb, b, pbb)
            P.op("dve", lambda e: e.tensor_reduce(out=sm[:, 0:1], in_=ps[:, b, 0:MEM], axis=AX.X, op=ALU.max, negate=True),
                 reads=[ps[:, b, 0:MEM]], writes=[sm[:, 0:1]])

        def xa2(it):
            sm, pfb, b, pbb = st_[it]
            v_ts(sm[:, 1:2], sm[:, 0:1], 1.0 / 16, ALU.mult)

        def xa3(it):
            sm, pfb, b, pbb = st_[it]
            act_fn(pfb[:, 0:MEM], ps[:, b, 0:MEM], AF.Exp, scale=1.0 / 16, bias=sm[:, 1:2], accum=sm[:, 2:3])

        def xb1(it):
            sm, pfb, b, pbb = st_[it]
            v_recip(sm[:, 3:4], sm[:, 2:3])

        def xb2(it):
            sm, pfb, b, pbb = st_[it]
            v_ts(pbb[:, 0:MEM], pfb[:, 0:MEM], sm[:, 3:4], ALU.mult)

        def xb3(it):
            c0, n, hx, bi = it
            sm, pfb, b, pbb = st_.pop(it)
            if bi == 0:
                st_[("pT", c0, hx)] = nxt("pT", pT)
            pTx = st_[("pT", c0, hx)]
            bt = bank()
            for mb in range(2):
                mm(ps[:, bt, mb * 128:(mb + 1) * 128], pbb[:, mb * 128:(mb + 1) * 128], identb, True, True)
            v_copy(pTx[:, :, bi * 128:(bi + 1) * 128], ps[:, bt, 0:256].rearrange("p (a c) -> p a c", a=2), eng="act")
            if bi == n // 128 - 1:
                for dc in range(2):
                    b2_ = bank()
                    for mb in range(2):
                        mm(ps[:, b2_, 0:n], vx[:, mb, (2 * hx + dc) * 128:(2 * hx + dc + 1) * 128], pTx[:, mb, 0:n],
                           mb == 0, mb == 1)
                    v_copy(u[:, 2 * hx + dc, c0:c0 + n], ps[:, b2_, 0:n], eng=("act" if dc else "dve"))
                del st_[("pT", c0, hx)]

        DEPTH = 3
        nit = len(items)
        xa0(items[0])
        for idx in range(nit + DEPTH):
            ia = items[idx] if idx < nit else None
            ib = items[idx - DEPTH] if 0 <= idx - DEPTH < nit else None
            if idx + 1 < nit:
                xa0(items[idx + 1])
            if ia:
                xa1(ia)
            if ib:
                xb1(ib)
            if ia:
                xa2(ia)
            if ib:
                xb2(ib)
            if ia:
                xa3(ia)
            if ib:
                xb3(ib)
        assert not st_
        for tj in range(4):
            wt = wtile(KC * 256)
            wv = wt[:, :].rearrange("p (k n) -> p k n", k=KC)
            wdma(wv, wsrc(wxo_d, 0, KC, tj * 256, 256))
            for mmi in range(2):
                m = tj * 2 + mmi
                for (c0, n) in main_subs:
                    b = bank()
                    for kc in range(KC):
                        mm(ps[:, b, 0:n], wv[:, kc, mmi * 128:(mmi + 1) * 128], u[:, kc, c0:c0 + n], kc == 0, kc == KC - 1)
                    v_tt(h[:, m, c0:c0 + n], ps[:, b, 0:n], h[:, m, c0:c0 + n], ALU.add)

    out_ops = []

    def final_out(p_idx, next_load):
        nrow0 = HALO + (p_idx + 1) * T
        pre = [x_dma(nrow0 + i * 128) for i in range(min(2, NBLK))] if next_load else []
        for bi in range(NBLK):
            stg = nxt("stage", stage)
            bks = [bank(), bank()]
            sm = nxt("small", small)
            for half in range(2):
                for q in range(4):
                    kc = half * 4 + q
                    tr(ps[:, bks[half], q * 128:(q + 1) * 128], h[:, kc, bi * 128:(bi + 1) * 128], identf[:, :])
                act_fn(stg[:, half * 512:(half + 1) * 512], ps[:, bks[half], :], AF.Square, accum=sm[:, half:half + 1])
            v_tt(sm[:, 2:3], sm[:, 0:1], sm[:, 1:2], ALU.add)
            act_fn(sm[:, 3:4], sm[:, 2:3], AF.Ln, scale=1.0 / D, bias=epsc[:, :])
            act_fn(sm[:, 2:3], sm[:, 3:4], AF.Exp, scale=-0.5)
            for half in range(2):
                v_stt(stg[:, half * 512:(half + 1) * 512], ps[:, bks[half], :], sm[:, 2:3],
                      gfin[:, half * 512:(half + 1) * 512], ALU.mult, ALU.mult)
            r0 = p_idx * T + bi * 128
            o = P.dma("sp", out_d.ap()[r0:r0 + 128, :], stg[:, :], reads=[stg[:, :]])
            out_ops.append(o)
            if next_load:
                x_tr(pre[bi], h, bi)
                if bi + 2 < NBLK:
                    pre.append(x_dma(nrow0 + (bi + 2) * 128))


    for p_idx in range(NPASS):
        main_subs = [(s * SUB, SUB) for s in range(NSUB)]
        halo = (p_idx == 0)
        if halo:
            load_x(0, 1, hh)
            load_x(HALO, NBLK, h)
            hook = lambda: rope_tables(0, TP, 0)
        else:
            v_copy(kA[:, 0:128], kA[:, T:T + 128])
            v_copy(kB[:, 0:128], kB[:, T:T + 128])
            v_copy(vtok[:, 0, :], vtok[:, NBLK, :])
            hook = (lambda pp: (lambda: rope_tables(128, T, HALO + pp * T)))(p_idx)
        subs = []
        if halo:
            subs.append((hh, uh, acth, HALO))
        for (c0, n) in main_subs:
            subs.append((h[:, :, c0:c0 + n], u[:, :, c0:c0 + n], act[:, :, c0:c0 + n], n))
        ffn(subs, GC_FFN1, w1i_d, w1o_d, hook=hook)
        mixer(p_idx, main_subs, halo)
        xattn(main_subs)
        subs2 = [(h[:, :, c0:c0 + n], u[:, :, c0:c0 + n], act[:, :, c0:c0 + n], n) for (c0, n) in main_subs]
        ffn(subs2, GC_FFN2, w2i_d, w2o_d)
        final_out(p_idx, p_idx + 1 < NPASS)

    P.finalize(sems)
    block = es.enter_context(nc.Block())

    @block.tensor
    def _(e):
        P.emit("pe", e)

    @block.scalar
    def _(e):
        P.emit("act", e)

    @block.vector
    def _(e):
        P.emit("dve", e)

    @block.gpsimd
    def _(e):
        P.emit("pool", e)

    @block.sync
    def _(e):
        P.emit("sp", e, final_wait=out_ops)

    es.close()
    return nc, P


def _host_consts():
    ident = np.eye(128, dtype=np.float32)
    idx = np.arange(128)
    perm32 = (idx // 64) * 64 + ((idx % 64) + 32) % 64
    p32 = np.zeros((128, 128), np.float32)
    p32[perm32, idx] = 1.0
    perm64 = (idx + 64) % 128
    p64 = np.zeros((128, 128), np.float32)
    p64[perm64, idx] = 1.0
    consts = np.ascontiguousarray(np.stack([ident, p32, p64], axis=1))
    qi = np.arange(128)[:, None]
    ki = np.arange(256)[None, :]
    rel = 128 + qi - ki
    band = (rel >= 0) & (rel < 128)
    m_std = np.where(band, 0.0, NEG).astype(np.float32)
    m_first0 = np.where(band & (ki >= 128), 0.0, NEG).astype(np.float32)
    half = 32
    inv_freq = (np.float32(10000.0) ** (-np.arange(half, dtype=np.float32) / np.float32(half))).astype(np.float32)
    return consts, m_std, m_first0, inv_freq


def make_in_maps(inputs, ntok, ncore_per_seq):
    x = np.asarray(inputs["x"], np.float32)
    mem = np.asarray(inputs["mem"], np.float32)
    pos = np.asarray(inputs["positions"], np.int32)
    B, S, _ = x.shape
    assert S == ntok * ncore_per_seq
    consts, m_std, m_first0, inv_freq = _host_consts()

    def sq(name):
        a = np.asarray(inputs[name], np.float32)
        return np.ascontiguousarray(a[0])

    def colmajor(v):
        v = np.asarray(v, np.float32)
        return v.reshape(-1, 128).T

    gains = np.zeros((128, GCOLS), np.float32)
    gains[:, GC_FFN1:GC_FFN1 + 8] = colmajor(sq("g_ffn1"))
    gains[:, GC_MIX:GC_MIX + 8] = colmajor(sq("g_mix"))
    gains[:, GC_XATT:GC_XATT + 8] = colmajor(sq("g_xattn"))
    gains[:, GC_FFN2:GC_FFN2 + 8] = colmajor(sq("g_ffn2"))
    ga = sq("g_attn_out")
    gains[:, GC_ATT:GC_ATT + 4] = ga.reshape(2, 4, 64).transpose(0, 2, 1).reshape(128, 4)
    gains[:, GC_CONVG:GC_CONVG + 4] = colmajor(sq("g_conv_out"))
    cw = sq("conv_w")
    for c in range(4):
        for tap in range(3):
            gains[:, GC_CONVW + c * 3 + tap] = cw[tap, c * 128:(c + 1) * 128]
    pidx = np.arange(128)
    gains[:, GC_INVF] = inv_freq[pidx % 32]
    gains[:, GC_SSCALE] = np.where((pidx % 64) < 32, -1.0, 1.0)
    gmem_bc = np.ascontiguousarray(np.broadcast_to(sq("g_mem")[None, :], (128, D)))
    gfin_bc = np.ascontiguousarray(np.broadcast_to(np.asarray(inputs["g_final"], np.float32)[None, :], (128, D)))
    sinks_bc = np.ascontiguousarray(np.broadcast_to(sq("sinks")[None, :], (128, 8)))
    shared = {
        "gains": gains, "gmem_bc": gmem_bc, "gfin_bc": gfin_bc, "sinks_bc": sinks_bc, "consts": consts,
        "w_ffn1_in": sq("w_ffn1_in"), "w_ffn1_out": sq("w_ffn1_out"), "w_mix_in": sq("w_mix_in"),
        "w_mix_out": sq("w_mix_out"), "w_xq": sq("w_xq"), "w_xkv": sq("w_xkv"), "w_xo": sq("w_xo"),
        "w_ffn2_in": sq("w_ffn2_in"), "w_ffn2_out": sq("w_ffn2_out"),
    }
    in_maps = []
    for b in range(B):
        for hf in range(ncore_per_seq):
            s0 = hf * ntok
            xc = np.zeros((HALO + ntok, D), np.float32)
            pc = np.zeros((HALO + ntok,), np.int32)
            xc[HALO:] = x[b, s0:s0 + ntok]
            pc[HALO:] = pos[b, s0:s0 + ntok]
            if hf > 0:
                xc[:HALO] = x[b, s0 - HALO:s0]
                pc[:HALO] = pos[b, s0 - HALO:s0]
            masks = np.ascontiguousarray(np.stack([m_first0 if hf == 0 else m_std, m_std], axis=1))
            m = dict(shared)
            m.update({"x": xc, "pos": np.ascontiguousarray(np.broadcast_to(pc[None, :], (128, HALO + ntok))),
                      "mem": np.ascontiguousarray(mem[b]), "masks": masks})
            in_maps.append(m)
    return in_maps


_NC_CACHE = {}


def kernel(**inputs):
    x = np.asarray(inputs["x"])
    B, S, _ = x.shape
    ncore = 8
    per_seq = ncore // B
    ntok = S // per_seq
    cfg = Cfg(ntok=ntok, T=1024, SUB=512)
    key = (ntok,)
    if key not in _NC_CACHE:
        _NC_CACHE[key] = build_program(cfg)[0]
    nc = _NC_CACHE[key]
    in_maps = make_in_maps(inputs, ntok, per_seq)
    res = run_bass_kernel_spmd(nc, in_maps, core_ids=list(range(ncore)))
    out = np.empty((B, S, D), np.float32)
    i = 0
    for b in range(B):
        for hf in range(per_seq):
            out[b, hf * ntok:(hf + 1) * ntok] = res.results[i]["out"]
            i += 1
    return out
```

```python
import contextlib
import numpy as np
import concourse.bass as bass
import concourse.mybir as mybir
from concourse.bass_utils import run_bass_kernel_spmd

F32 = mybir.dt.float32
BF16 = mybir.dt.bfloat16
I32 = mybir.dt.int32
AF = mybir.ActivationFunctionType
ALU = mybir.AluOpType
AX = mybir.AxisListType

D = 1024
KC = 8
DFF = 2816
FC = 22
HALO = 128
MEM = 256
RMS_EPS = 1e-5
NEG = -30000.0
PI = float(np.pi)
TWO_PI = float(2 * np.pi)

GC_FFN1, GC_MIX, GC_XATT, GC_FFN2 = 0, 8, 16, 24
GC_ATT, GC_CONVG, GC_CONVW, GC_INVF, GC_SSCALE = 32, 36, 40, 52, 53
GCOLS = 56


class Cfg:
    def __init__(self, ntok=4096, T=1024, SUB=512):
        self.ntok, self.T, self.SUB = ntok, T, SUB
        self.npass = ntok // T
        self.nsub = T // SUB
        self.nblk = T // 128
        assert ntok % T == 0 and T % SUB == 0 and SUB % 128 == 0


def dsize(dt):
    return 2 if dt == BF16 else 4


class Op:
    __slots__ = ("eng", "fn", "deps", "sem", "val", "signals", "is_dma", "seq")
    _n = 0

    def __init__(self, eng, fn):
        self.eng, self.fn = eng, fn
        Op._n += 1
        self.seq = Op._n
        self.deps = []
        self.sem = None
        self.val = None
        self.signals = False
        self.is_dma = False


class Prog:
    CELL = 256
    ENGS = ("pe", "act", "dve", "pool", "sp")

    def __init__(self, nc):
        self.nc = nc
        self.ops = {e: [] for e in self.ENGS}
        self.bufs = {}
        self.cells = {}
        self.cache = {}
        self.dma_sems = {}
        self.dma_rr = {e: 0 for e in self.ENGS}
        self.sb_next = 16512
        self.sb_end = 229376

    def sb(self, name, shape, dtype, alias=None):
        size = int(np.prod(shape[1:])) * dsize(dtype)
        if alias is None:
            addr = (self.sb_next + 255) // 256 * 256
            self.sb_next = addr + size
            assert self.sb_next <= self.sb_end, f"SBUF overflow at {name}: {self.sb_next}"
        else:
            addr = alias
            assert addr % 32 == 0
        t = self.nc.alloc_sbuf_tensor_at(name, list(shape), dtype, offset=addr)
        self.bufs[t.name] = ("sb", addr, dsize(dtype))
        return t, addr, size

    def reg_psum(self, t):
        self.bufs[t.name] = ("ps", 0, 4)

    def _cells(self, ap):
        key = (ap.tensor.name, ap.offset, ap.ap)
        r = self.cache.get(key)
        if r is not None:
            return r
        space, base, ds = self.bufs[ap.tensor.name]
        dims = ap.ap
        pstep = dims[0][0]
        fo = ap.offset % pstep if pstep > 0 else ap.offset
        free = dims[1:]
        if not free:
            free = ((1, 1),)
        last_step, last_n = free[-1]
        starts = [fo]
        for (st, n) in free[:-1]:
            starts = [s + i * st for s in starts for i in range(n)]
        span = (last_n - 1) * abs(last_step) + 1
        csz = 2048 if space == "ps" else self.CELL
        cs = set()
        for s in starts:
            lo = base + s * ds
            hi = base + (s + span) * ds
            for c in range(lo // csz, (hi - 1) // csz + 1):
                cs.add((space, c))
        r = tuple(cs)
        self.cache[key] = r
        return r

    def _track(self, op, reads, writes):
        key = id(op) if op.is_dma else op.eng
        cands = {}

        def cand(other):
            if other is None or other is op:
                return
            if other.is_dma:
                cands[id(other)] = other
                return
            if other.eng == "pe" and op.eng == "pe" and not op.is_dma:
                return
            cur = cands.get(other.eng)
            if cur is None or cur.seq < other.seq:
                cands[other.eng] = other

        for ap in reads:
            for c in self._cells(ap):
                st = self.cells.get(c)
                if st is None:
                    st = [None, {}]
                    self.cells[c] = st
                cand(st[0])
                if c[0] == "ps":
                    for r in st[1].values():
                        if r.eng != op.eng:
                            cand(r)
                st[1][key] = op
        for ap in writes:
            for c in self._cells(ap):
                st = self.cells.get(c)
                if st is None:
                    st = [None, {}]
                    self.cells[c] = st
                cand(st[0])
                for r in st[1].values():
                    cand(r)
                st[0] = op
                st[1] = {}
        for other in cands.values():
            op.deps.append(other)
            other.signals = True

    def op(self, eng, fn, reads=(), writes=()):
        o = Op(eng, fn)
        self._track(o, reads, writes)
        self.ops[eng].append(o)
        return o

    def dma(self, eng, out, in_, reads=(), writes=()):
        o = Op(eng, None)
        o.is_dma = True
        pool = self.dma_sems[eng]
        slot = pool[self.dma_rr[eng] % len(pool)]
        self.dma_rr[eng] += 1
        if slot[2] is not None:
            o.deps.append(slot[2])
        slot[1] += 16
        slot[2] = o
        o.sem, o.val = slot[0], slot[1]
        o.signals = True
        o.fn = (out, in_)
        self._track(o, reads, writes)
        self.ops[eng].append(o)
        return o

    def finalize(self, sems):
        for e in self.ENGS:
            n = 0
            for o in self.ops[e]:
                if o.is_dma:
                    continue
                o.sem = sems[e]
                if o.signals:
                    n += 1
                    o.val = n

    def emit(self, eng, h, final_wait=()):
        seen = {}
        for o in self.ops[eng]:
            need = {}
            for d in o.deps:
                assert d.val is not None
                k = d.sem
                if seen.get(k.num, 0) >= d.val:
                    continue
                if need.get(k.num, (None, 0))[1] < d.val:
                    need[k.num] = (k, d.val)
            for num, (k, v) in need.items():
                h.wait_ge(k, v)
                seen[num] = v
            if o.is_dma:
                out, in_ = o.fn
                h.dma_start(out=out, in_=in_).then_inc(o.sem, 16)
            else:
                ins = o.fn(h)
                if o.signals:
                    ins.then_inc(o.sem, 1)
        for d in final_wait:
            if seen.get(d.sem.num, 0) < d.val:
                h.wait_ge(d.sem, d.val)
                seen[d.sem.num] = d.val


def build_program(cfg):
    nc = bass.Bass("TRN2", target_bir_lowering=False)
    P = Prog(nc)
    es = contextlib.ExitStack()
    sems = {e: es.enter_context(nc.semaphore(f's_{e}')) for e in Prog.ENGS}
    P.dma_sems = {'pool': [[es.enter_context(nc.semaphore(f'dp{i}')), 0, None] for i in range(16)],
                  'sp': [[es.enter_context(nc.semaphore(f'ds{i}')), 0, None] for i in range(8)]}
    T, SUB, NSUB, NBLK, NPASS = cfg.T, cfg.SUB, cfg.nsub, cfg.nblk, cfg.npass
    NTOT = HALO + cfg.ntok
    TP = T + 128

    def din(name, shape, dt=F32):
        return nc.dram_tensor(name, list(shape), dt, kind="ExternalInput")

    x_d = din("x", [NTOT, D])
    pos_d = din("pos", [128, NTOT], I32)
    mem_d = din("mem", [MEM, D])
    gains_d = din("gains", [128, GCOLS])
    gmem_d = din("gmem_bc", [128, D])
    gfin_d = din("gfin_bc", [128, D])
    sinks_d = din("sinks_bc", [128, 8])
    consts_d = din("consts", [128, 3, 128])
    masks_d = din("masks", [128, 2, 256])
    w1i_d = din("w_ffn1_in", [D, 2 * DFF])
    w1o_d = din("w_ffn1_out", [DFF, D])
    wmi_d = din("w_mix_in", [D, 2304])
    wmo_d = din("w_mix_out", [D, D])
    wxq_d = din("w_xq", [D, D])
    wxkv_d = din("w_xkv", [D, 2 * D])
    wxo_d = din("w_xo", [D, D])
    w2i_d = din("w_ffn2_in", [D, 2 * DFF])
    w2o_d = din("w_ffn2_out", [DFF, D])
    out_d = nc.dram_tensor("out", [cfg.ntok, D], F32, kind="ExternalOutput")

    h, _, _ = P.sb("h", [128, KC, T], F32)
    u, _, _ = P.sb("u", [128, KC, T], BF16)
    bufA, bufA_addr, bufA_size = P.sb("bufA", [128, KC, max(T, 1024)], BF16)
    hh, _, _ = P.sb("hh", [128, KC, HALO], F32, alias=bufA_addr)
    uh, _, _ = P.sb("uh", [128, KC, HALO], BF16, alias=bufA_addr + 4096)
    acth, _, _ = P.sb("acth", [128, FC, HALO], BF16, alias=bufA_addr + 4096 + 2048)
    assert 4096 + 2048 + FC * HALO * 2 <= bufA_size
    kA, _, _ = P.sb("kA", [128, TP], BF16)
    kB, _, _ = P.sb("kB", [128, TP], BF16)
    vtok, _, _ = P.sb("vtok", [128, NBLK + 1, 128], BF16)
    zprev, _, _ = P.sb("zprev", [128, 4, 2], F32)
    kx, _, _ = P.sb("kx", [128, KC, MEM], BF16)
    vx, _, _ = P.sb("vx", [128, 2, D], BF16)
    gains, _, _ = P.sb("gains", [128, GCOLS], F32)
    sink8, _, _ = P.sb("sink8", [128, 8], F32)
    identf, _, _ = P.sb("identf", [128, 128], F32)
    cbf, _, _ = P.sb("cbf", [128, 4, 128], BF16)
    masks, _, _ = P.sb("masks", [128, 2, 256], F32)
    epsc, _, _ = P.sb("epsc", [128, 1], F32)
    maskb, _, _ = P.sb("maskb", [128, 2, 256], BF16)
    sinkb, _, _ = P.sb("sinkb", [128, 8], BF16)
    gfin, _, _ = P.sb("gfin", [128, D], F32)
    stage = [P.sb(f"stage{i}", [128, D], F32)[0] for i in range(2)]
    xstage = [P.sb(f"xstage{i}", [128, D], F32)[0] for i in range(2)]
    rstd = [P.sb(f"rstd{i}", [128, SUB], F32)[0] for i in range(2)]
    lnv = [P.sb(f"lnv{i}", [128, SUB], F32)[0] for i in range(2)]
    small = [P.sb(f"small{i}", [128, 4], F32)[0] for i in range(8)]
    sgb = [P.sb(f"sgb{i}", [128, SUB], F32)[0] for i in range(2)]
    tmpb = [P.sb(f"tmpb{i}", [128, max(SUB, 512)], BF16)[0] for i in range(2)]
    act, scr_addr, act_size = P.sb("act", [128, FC, T], BF16)
    a = [scr_addr]

    def scr(name, shape, dt):
        t_, _, sz = P.sb(name, shape, dt, alias=a[0])
        a[0] += (sz + 255) // 256 * 256
        return t_

    qr = scr("qr", [128, 4, T], BF16)
    pT = [scr(f"pT{i}", [128, 2, 512], BF16) for i in range(2)]
    zb2 = [scr(f"zb{i}", [128, SUB + 2], F32) for i in range(2)]
    yb_addr = a[0]
    ybs = [scr(f"yb{i}", [128, 4, SUB], F32) for i in range(NSUB)]
    if 4 * 1280 + 4 * 512 <= NSUB * 4 * SUB * 4:
        pf = [P.sb(f"pf{i}", [128, 264], F32, alias=yb_addr + i * 1280)[0] for i in range(4)]
        pb = [P.sb(f"pb{i}", [128, 256], BF16, alias=yb_addr + 4 * 1280 + i * 512)[0] for i in range(4)]
    else:
        pf = [scr(f"pf{i}", [128, 264], F32) for i in range(4)]
        pb = [scr(f"pb{i}", [128, 256], BF16) for i in range(4)]
    ssb = scr("ssb", [128, 8, 264], F32)
    assert a[0] <= scr_addr + act_size or T < 1024, (a[0] - scr_addr, act_size)
    a[0] = max(a[0], scr_addr + act_size)
    cos2 = scr("cos2", [128, TP], F32)
    sinS = scr("sinS", [128, TP], F32)
    tmpf_addr = a[0]
    tmpf = [scr(f"tmpf{i}", [128, SUB], F32) for i in range(3)]
    scr_end = a[0]
    RCH = ((3 * SUB) // 256 * 256) // 4
    a[0] = tmpf_addr
    posi = scr("posi", [128, RCH], I32)
    posf = scr("posf", [128, RCH], F32)
    ang = scr("ang", [128, RCH], F32)
    kf = scr("kf", [128, RCH], F32)
    assert a[0] <= scr_end
    P.sb_next = max(P.sb_next, scr_end)
    memt = [P.sb(f"memt{i}", [128, D], F32, alias=scr_addr + i * 4096)[0] for i in range(2)]
    memn = [P.sb(f"memn{i}", [128, D], F32, alias=scr_addr + 8192 + i * 4096)[0] for i in range(2)]
    memT, _, _ = P.sb("memT", [128, KC, MEM], BF16, alias=scr_addr + 16384)
    cstage, _, _ = P.sb("cstage", [128, 3, 128], F32, alias=scr_addr + 16384 + 4096)
    ring_addr = (P.sb_next + 255) // 256 * 256
    ring_size = (P.sb_end - ring_addr) // 256 * 256
    assert ring_size >= 24 * 1024, f"ring too small {ring_size}"
    ring_pos = [0]
    wcount = [0]

    def wtile(nfree):
        size = (nfree * 2 + 255) // 256 * 256
        if ring_pos[0] + size > ring_size:
            ring_pos[0] = 0
        addr = ring_addr + ring_pos[0]
        ring_pos[0] += size
        wcount[0] += 1
        t_ = nc.alloc_sbuf_tensor_at(f"wt{wcount[0]}", [128, nfree], BF16, offset=addr)
        P.bufs[t_.name] = ("sb", addr, 2)
        return t_

    ps = nc.alloc_psum_tensor("ps", [128, 8, 512], F32)
    P.reg_psum(ps)
    bank_rr = [0]

    def bank():
        b = bank_rr[0] % 6
        bank_rr[0] += 1
        return b

    sb_rr = [0, 0]

    def sbank():
        sb_rr[0] += 1
        return sb_rr[0] % 3

    def obank():
        sb_rr[1] += 1
        return 3 + sb_rr[1] % 3

    def wdma(dst, src):
        return P.dma("pool", dst, src, writes=[dst])

    def ldma(dst, src):
        return P.dma("sp", dst, src, writes=[dst])

    def mm(out, lhsT, rhs, start, stop):
        return P.op("pe", lambda e: e.matmul(out, lhsT=lhsT, rhs=rhs, start=start, stop=stop),
                    reads=[lhsT, rhs], writes=[out])

    def tr(out, in_, ident):
        return P.op("pe", lambda e: e.transpose(out, in_, ident), reads=[in_, ident], writes=[out])

    def act_fn(out, in_, func, scale=1.0, bias=None, accum=None):
        reads = [in_]
        kw = {}
        if bias is not None:
            kw["bias"] = bias
            reads.append(bias)
        if not isinstance(scale, float):
            reads.append(scale)
        writes = [out]
        if accum is not None:
            kw["accum_out"] = accum
            writes.append(accum)
        return P.op("act", lambda e: e.activation(out=out, in_=in_, func=func, scale=scale, **kw),
                    reads=reads, writes=writes)

    def v_tt(out, in0, in1, op):
        return P.op("dve", lambda e: e.tensor_tensor(out=out, in0=in0, in1=in1, op=op),
                    reads=[in0, in1], writes=[out])

    def v_ts(out, in0, s1, op0, s2=None, op1=None):
        reads = [in0] + [s for s in (s1, s2) if s is not None and not isinstance(s, float)]
        if op1 is None:
            return P.op("dve", lambda e: e.tensor_scalar(out=out, in0=in0, scalar1=s1, scalar2=None, op0=op0),
                        reads=reads, writes=[out])
        return P.op("dve", lambda e: e.tensor_scalar(out=out, in0=in0, scalar1=s1, scalar2=s2, op0=op0, op1=op1),
                    reads=reads, writes=[out])

    def v_stt(out, in0, scalar, in1, op0, op1):
        reads = [in0, in1] + ([] if isinstance(scalar, float) else [scalar])
        return P.op("dve", lambda e: e.scalar_tensor_tensor(out=out, in0=in0, scalar=scalar, in1=in1, op0=op0, op1=op1),
                    reads=reads, writes=[out])

    def v_copy(out, in_, eng="dve"):
        if eng == "act":
            return P.op("act", lambda e: e.copy(out=out, in_=in_), reads=[in_], writes=[out])
        return P.op("dve", lambda e: e.tensor_copy(out=out, in_=in_), reads=[in_], writes=[out])

    def v_max(out, in_):
        return P.op("dve", lambda e: e.tensor_reduce(out=out, in_=in_, axis=AX.X, op=ALU.max),
                    reads=[in_], writes=[out])

    def v_recip(out, in_):
        return P.op("dve", lambda e: e.reciprocal(out=out, in_=in_), reads=[in_], writes=[out])

    def v_memset(ap, val):
        return P.op("dve", lambda e: e.memset(ap, val), writes=[ap])

    def wsrc(wd, r0, nr, c0, ncol):
        return wd.ap()[r0:r0 + nr * 128, c0:c0 + ncol].rearrange("(kc p) n -> p kc n", p=128)

    ones = cbf[:, 0, :]
    p32 = cbf[:, 1, :]
    p64 = cbf[:, 2, :]
    identb = cbf[:, 3, :]
    rr = {}

    def nxt(key, lst):
        i = rr.get(key, 0)
        rr[key] = i + 1
        return lst[i % len(lst)]

    ldma(gains[:, :], gains_d.ap())
    ldma(cstage[:, :, :], consts_d.ap())
    ldma(masks[:, :, :], masks_d.ap())
    ldma(gfin[:, :], gfin_d.ap())
    v_memset(epsc[:, :], RMS_EPS)
    v_memset(cbf[:, 0, :], 1.0)
    v_copy(identf[:, :], cstage[:, 0, :])
    v_copy(cbf[:, 1, :], cstage[:, 1, :])
    v_copy(cbf[:, 2, :], cstage[:, 2, :])
    v_copy(cbf[:, 3, :], cstage[:, 0, :])
    v_memset(zprev[:, :, :], 0.0)
    v_copy(maskb[:, :, :], masks[:, :, :])
    P.dma("pool", sinkb[:, :], sinks_d.ap(), writes=[sinkb[:, :]])

    def rms_stats_fm(src_chunks, n, nfeat):
        b = bank()
        ssp = ps[:, b, 0:n]
        nchunk = len(src_chunks)
        for i, s in enumerate(src_chunks):
            sq = nxt("tmpb", tmpb)[:, 0:n]
            act_fn(sq, s, AF.Square)
            mm(ssp, ones, sq, i == 0, i == nchunk - 1)
        i_ = rr.get("rstd", 0)
        rr["rstd"] = i_ + 1
        lv = lnv[i_ % 2][:, 0:n]
        rs = rstd[i_ % 2][:, 0:n]
        act_fn(lv, ssp, AF.Ln, scale=1.0 / nfeat, bias=epsc[:, :])
        act_fn(rs, lv, AF.Exp, scale=-0.5)
        return rs

    def rmsnorm_to_u(hsrc, udst, n, gcol):
        rs = rms_stats_fm([hsrc[:, kc, :] for kc in range(KC)], n, D)
        for kc in range(KC):
            v_stt(udst[:, kc, :], hsrc[:, kc, :], gains[:, gcol + kc:gcol + kc + 1], rs, ALU.mult, ALU.mult)

    ldma(memt[0][:, :], mem_d.ap()[0:128, :])
    ldma(memt[1][:, :], mem_d.ap()[128:256, :])
    gmem_sb = stage[0]
    ldma(gmem_sb[:, :], gmem_d.ap())
    for mb in range(2):
        ssq = small[0][:, mb:mb + 1]
        act_fn(stage[1][:, :], memt[mb][:, :], AF.Square, accum=ssq)
        lv_ = small[1][:, mb:mb + 1]
        act_fn(lv_, ssq, AF.Ln, scale=1.0 / D, bias=epsc[:, :])
        rs_ = small[2][:, mb:mb + 1]
        act_fn(rs_, lv_, AF.Exp, scale=-0.5)
        v_stt(memn[mb][:, :], memt[mb][:, :], rs_, gmem_sb[:, :], ALU.mult, ALU.mult)
        for half in range(2):
            b = bank()
            for q in range(4):
                kc = half * 4 + q
                tr(ps[:, b, q * 128:(q + 1) * 128], memn[mb][:, kc * 128:(kc + 1) * 128], identf[:, :])
            v_copy(memT[:, half * 4:half * 4 + 4, mb * 128:(mb + 1) * 128],
                   ps[:, b, :].rearrange("p (a c) -> p a c", a=4), eng=("act" if half else "dve"))
    for ti in range(4):
        wt = wtile(KC * 256)
        wv = wt[:, :].rearrange("p (k n) -> p k n", k=KC)
        wdma(wv, wsrc(wxkv_d, 0, KC, ti * 256, 256))
        for cc in range(2):
            oc = ti * 2 + cc
            b = bank()
            for kc in range(KC):
                mm(ps[:, b, 0:MEM], wv[:, kc, cc * 128:(cc + 1) * 128], memT[:, kc, :], kc == 0, kc == KC - 1)
            v_copy(kx[:, oc, :], ps[:, b, 0:MEM], eng=("act" if cc else "dve"))
    for ti in range(2):
        wt = wtile(KC * 512)
        wv = wt[:, :].rearrange("p (k n) -> p k n", k=KC)
        wdma(wv, wsrc(wxkv_d, 0, KC, D + ti * 512, 512))
        for mb in range(2):
            b = bank()
            for kc in range(KC):
                mm(ps[:, b, :], memT[:, kc, mb * 128:(mb + 1) * 128], wv[:, kc, :], kc == 0, kc == KC - 1)
            v_copy(vx[:, mb, ti * 512:(ti + 1) * 512], ps[:, b, :], eng=("act" if mb else "dve"))

    def x_dma(row0):
        stg = nxt("xstage", xstage)
        ldma(stg[:, :], x_d.ap()[row0:row0 + 128, :])
        return stg

    def x_tr(stg, hdst, bi):
        for half in range(2):
            b = bank()
            for q in range(4):
                kc = half * 4 + q
                tr(ps[:, b, q * 128:(q + 1) * 128], stg[:, kc * 128:(kc + 1) * 128], identf[:, :])
            v_copy(hdst[:, half * 4:half * 4 + 4, bi * 128:(bi + 1) * 128],
                   ps[:, b, :].rearrange("p (a c) -> p a c", a=4), eng=("act" if half else "dve"))

    def load_x(row0, nblocks, hdst):
        for bi in range(nblocks):
            x_tr(x_dma(row0 + bi * 128), hdst, bi)

    def ffn(subs, gcol, wi_d, wo_d, hook=None):
        for (hs, us, as_, n) in subs:
            rmsnorm_to_u(hs, us, n, gcol)
        hooks = list(hook()) if hook is not None else []
        for ti in range(FC // 2):
            wt = wtile(KC * 512)
            wv = wt[:, :].rearrange("p (k n) -> p k n", k=KC)
            wdma(wv[:, :, 0:256], wsrc(wi_d, 0, KC, ti * 256, 256))
            wdma(wv[:, :, 256:512], wsrc(wi_d, 0, KC, DFF + ti * 256, 256))
            for cc in range(2):
                c = ti * 2 + cc
                for (hs, us, as_, n) in subs:
                    bg, bu = bank(), bank()
                    for kc in range(KC):
                        mm(ps[:, bg, 0:n], wv[:, kc, cc * 128:(cc + 1) * 128], us[:, kc, :], kc == 0, kc == KC - 1)
                    for kc in range(KC):
                        mm(ps[:, bu, 0:n], wv[:, kc, 256 + cc * 128:256 + (cc + 1) * 128], us[:, kc, :], kc == 0, kc == KC - 1)
                    sg = nxt("sgb", sgb)[:, 0:n]
                    act_fn(sg, ps[:, bg, 0:n], AF.Silu)
                    v_tt(as_[:, c, :], sg, ps[:, bu, 0:n], ALU.mult)
            if ti >= 1 and hooks:
                hooks.pop(0)()
        while hooks:
            hooks.pop(0)()
        for tj in range(4):
            wt = wtile(FC * 256)
            wv = wt[:, :].rearrange("p (k n) -> p k n", k=FC)
            wdma(wv[:, 0:11, :], wsrc(wo_d, 0, 11, tj * 256, 256))
            wdma(wv[:, 11:22, :], wsrc(wo_d, 11 * 128, 11, tj * 256, 256))
            for mmi in range(2):
                m = tj * 2 + mmi
                for (hs, us, as_, n) in subs:
                    b = bank()
                    for kc in range(FC):
                        mm(ps[:, b, 0:n], wv[:, kc, mmi * 128:(mmi + 1) * 128], as_[:, kc, :], kc == 0, kc == FC - 1)
                    v_stt(hs[:, m, :], ps[:, b, 0:n], 0.5, hs[:, m, :], ALU.mult, ALU.add)

    def rope_tables(col0, ncols, pos_col0):
        thunks = []
        for o in range(0, ncols, RCH):
            w = min(RCH, ncols - o)
            sl = slice(col0 + o, col0 + o + w)
            tl = slice(0, w)

            def part(which, sl=sl, tl=tl, o=o, w=w):
                if which == 0:
                    ldma(posi[:, tl], pos_d.ap()[:, pos_col0 + o:pos_col0 + o + w])
                    v_copy(posf[:, tl], posi[:, tl])
                    v_ts(ang[:, tl], posf[:, tl], gains[:, GC_INVF:GC_INVF + 1], ALU.mult)
                dst = sinS if which == 0 else cos2
                src_ = ang
                if which == 1:
                    v_ts(posf[:, tl], ang[:, tl], PI / 2, ALU.add)
                    src_ = posf
                v_ts(kf[:, tl], src_[:, tl], 1.0 / TWO_PI, ALU.mult)
                v_copy(posi[:, tl], kf[:, tl])
                v_copy(kf[:, tl], posi[:, tl])
                C1 = 6.28125
                C2 = TWO_PI - C1
                v_stt(dst[:, sl], kf[:, tl], -C1, src_[:, tl], ALU.mult, ALU.add)
                v_stt(dst[:, sl], kf[:, tl], -C2, dst[:, sl], ALU.mult, ALU.add)
                v_ts(kf[:, tl], dst[:, sl], PI, ALU.is_gt)
                v_stt(dst[:, sl], kf[:, tl], -TWO_PI, dst[:, sl], ALU.mult, ALU.add)
                v_ts(kf[:, tl], dst[:, sl], -PI, ALU.is_lt)
                v_stt(dst[:, sl], kf[:, tl], TWO_PI, dst[:, sl], ALU.mult, ALU.add)
                v_ts(dst[:, sl], dst[:, sl], PI, ALU.min, -PI, ALU.max)
                if which == 0:
                    act_fn(dst[:, sl], dst[:, sl], AF.Sin, scale=gains[:, GC_SSCALE:GC_SSCALE + 1])
                else:
                    act_fn(dst[:, sl], dst[:, sl], AF.Sin)

            thunks.append(lambda part=part: part(0))
            thunks.append(lambda part=part: part(1))
        return thunks

    def rope_apply(psrc_bank, n, tcol0, dst_ap, pre=None):
        qb = nxt("tmpb", tmpb)[:, 0:n]
        v_copy(qb, ps[:, psrc_bank, 0:n], eng="act")
        b2 = bank()
        mm(ps[:, b2, 0:n], p32, qb, True, True)
        t1 = nxt("tmpf", tmpf)[:, 0:n]
        t2 = nxt("tmpf", tmpf)[:, 0:n]
        if pre is None:
            v_tt(t1, ps[:, psrc_bank, 0:n], cos2[:, tcol0:tcol0 + n], ALU.mult)
            v_tt(t2, ps[:, b2, 0:n], sinS[:, tcol0:tcol0 + n], ALU.mult)
        else:
            v_stt(t1, ps[:, psrc_bank, 0:n], pre, cos2[:, tcol0:tcol0 + n], ALU.mult, ALU.mult)
            v_stt(t2, ps[:, b2, 0:n], pre, sinS[:, tcol0:tcol0 + n], ALU.mult, ALU.mult)
        v_tt(dst_ap, t1, t2, ALU.add)

    def mixer(p_idx, main_subs, halo):
        if halo:
            rmsnorm_to_u(hh, uh, HALO, GC_MIX)
        for (c0, n) in main_subs:
            rmsnorm_to_u(h[:, :, c0:c0 + n], u[:, :, c0:c0 + n], n, GC_MIX)
        toks = []
        if halo:
            toks.append((uh, HALO, 0, True, None))
        for (c0, n) in main_subs:
            toks.append((u[:, :, c0:c0 + n], n, 128 + c0, False, c0))
        for ti in range(2):
            wt = wtile(KC * 256)
            wv = wt[:, :].rearrange("p (k n) -> p k n", k=KC)
            wdma(wv, wsrc(wmi_d, 0, KC, ti * 256, 256))
            for cc in range(2):
                c = ti * 2 + cc
                for (us, n, tc0, ish, c0) in toks:
                    if ish:
                        continue
                    b = bank()
                    for kc in range(KC):
                        mm(ps[:, b, 0:n], wv[:, kc, cc * 128:(cc + 1) * 128], us[:, kc, :], kc == 0, kc == KC - 1)
                    rope_apply(b, n, tc0, qr[:, c, c0:c0 + n], pre=0.125)
        wt = wtile(KC * 256)
        wv = wt[:, :].rearrange("p (k n) -> p k n", k=KC)
        wdma(wv, wsrc(wmi_d, 0, KC, 512, 256))
        for (us, n, tc0, ish, c0) in toks:
            b = bank()
            for kc in range(KC):
                mm(ps[:, b, 0:n], wv[:, kc, 0:128], us[:, kc, :], kc == 0, kc == KC - 1)
            rope_apply(b, n, tc0, kA[:, tc0:tc0 + n])
            b3 = bank()
            mm(ps[:, b3, 0:n], p64, kA[:, tc0:tc0 + n], True, True)
            v_copy(kB[:, tc0:tc0 + n], ps[:, b3, 0:n], eng="act")
            b4 = bank()
            nb_ = n // 128
            for bi in range(nb_):
                for kc in range(KC):
                    mm(ps[:, b4, bi * 128:(bi + 1) * 128], us[:, kc, bi * 128:(bi + 1) * 128], wv[:, kc, 128:256],
                       kc == 0, kc == KC - 1)
            blk0 = tc0 // 128
            v_copy(vtok[:, blk0:blk0 + nb_, :], ps[:, b4, 0:n].rearrange("p (a c) -> p a c", a=nb_))
        for c in range(4):
            wt = wtile(KC * 384)
            wv = wt[:, :].rearrange("p (k n) -> p k n", k=KC)
            wdma(wv[:, :, 0:128], wsrc(wmi_d, 0, KC, 768 + c * 128, 128))
            wdma(wv[:, :, 128:256], wsrc(wmi_d, 0, KC, 1280 + c * 128, 128))
            wdma(wv[:, :, 256:384], wsrc(wmi_d, 0, KC, 1792 + c * 128, 128))
            for (us, n, tc0, ish, c0) in toks:
                bgc, bxc = bank(), bank()
                for kc in range(KC):
                    mm(ps[:, bgc, 0:n], wv[:, kc, 128:256], us[:, kc, :], kc == 0, kc == KC - 1)
                for kc in range(KC):
                    mm(ps[:, bxc, 0:n], wv[:, kc, 256:384], us[:, kc, :], kc == 0, kc == KC - 1)
                xs = nxt("tmpf", tmpf)[:, 0:n]
                v_copy(xs, ps[:, bxc, 0:n], eng="act")
                if ish:
                    zt = nxt("tmpf", tmpf)[:, 0:n]
                    v_tt(zt, ps[:, bgc, 0:n], xs, ALU.mult)
                    v_copy(zprev[:, c, :], zt[:, n - 2:n])
                    continue
                bgb = bank()
                for kc in range(KC):
                    mm(ps[:, bgb, 0:n], wv[:, kc, 0:128], us[:, kc, :], kc == 0, kc == KC - 1)
                sidx = c0 // SUB
                zz = nxt("zb", zb2)
                v_copy(zz[:, 0:2], zprev[:, c, :])
                v_tt(zz[:, 2:2 + n], ps[:, bgc, 0:n], xs, ALU.mult)
                v_copy(zprev[:, c, :], zz[:, n:n + 2])
                cv = nxt("tmpf", tmpf)[:, 0:n]
                gw = GC_CONVW + c * 3
                v_ts(cv, zz[:, 2:2 + n], gains[:, gw + 2:gw + 3], ALU.mult)
                v_stt(cv, zz[:, 1:1 + n], gains[:, gw + 1:gw + 2], cv, ALU.mult, ALU.add)
                v_stt(cv, zz[:, 0:n], gains[:, gw:gw + 1], cv, ALU.mult, ALU.add)
                v_tt(ybs[sidx][:, c, 0:n], cv, ps[:, bgb, 0:n], ALU.mult)
                if c == 3:
                    rs = rms_stats_fm([ybs[sidx][:, cc_, 0:n] for cc_ in range(4)], n, 512)
                    for cc_ in range(4):
                        v_stt(bufA[:, 4 + cc_, c0:c0 + n], ybs[sidx][:, cc_, 0:n],
                              gains[:, GC_CONVG + cc_:GC_CONVG + cc_ + 1], rs, ALU.mult, ALU.mult)
        items = [(bi, g, j) for bi in range(NBLK) for g in range(2) for j in range(4)]
        st_ = {}

        def a0(it):
            bi, g, j = it
            hd = 4 * g + j
            c = hd // 2
            half = hd % 2
            pl = slice(half * 64, half * 64 + 64)
            kk = kA if half == g else kB
            gblk = p_idx * NBLK + bi
            mk = maskb[:, 0, :] if gblk == 0 else maskb[:, 1, :]
            b = sbank()
            mm(ps[:, b, 0:256], qr[pl, c, bi * 128:(bi + 1) * 128], kk[pl, bi * 128:bi * 128 + 256], True, False)
            mm(ps[:, b, 0:256], identb, mk, False, False)
            mm(ps[:, b, 256:257], identb, sinkb[:, hd:hd + 1], False, True)
            st_[("S", it)] = b

        def a1(it):
            b = st_.pop(("S", it))
            sm = nxt("small", small)
            st_[it] = [sm, nxt("pf", pf), nxt("pb", pb), b]
            P.op("dve", lambda e: e.tensor_reduce(out=sm[:, 0:1], in_=ps[:, b, 0:257], axis=AX.X, op=ALU.max, negate=True),
                 reads=[ps[:, b, 0:257]], writes=[sm[:, 0:1]])

        def a2(it):
            pass

        def a3(it):
            sm, pfb, pbb, b = st_[it]
            act_fn(pfb[:, 0:257], ps[:, b, 0:257], AF.Exp, bias=sm[:, 0:1], accum=sm[:, 2:3])
            st_[it] = [sm, pfb, pbb]

        def b1(it):
            sm, pfb, pbb = st_[it]
            v_recip(sm[:, 3:4], sm[:, 2:3])

        def b2(it):
            sm, pfb, pbb = st_[it]
            v_ts(pbb[:, 0:256], pfb[:, 0:256], sm[:, 3:4], ALU.mult)

        def b3(it):
            bi, g, j = it
            sm, pfb, pbb = st_.pop(it)
            if j == 0:
                st_[("pT", bi, g)] = nxt("pT", pT)
            pTg = st_[("pT", bi, g)]
            bt = obank()
            for kb in range(2):
                mm(ps[:, bt, kb * 128:(kb + 1) * 128], pbb[:, kb * 128:(kb + 1) * 128], identb, True, True)
            v_copy(pTg[:, :, j * 128:(j + 1) * 128], ps[:, bt, 0:256].rearrange("p (a c) -> p a c", a=2), eng="act")
            gblk = p_idx * NBLK + bi
            bo = 6 + (gblk % 2)
            if j == 3:
                for kb in range(2):
                    mm(ps[g * 64:(g + 1) * 64, bo, :], vtok[:, bi + kb, g * 64:(g + 1) * 64], pTg[:, kb, :],
                       kb == 0, kb == 1)
                del st_[("pT", bi, g)]
            if j == 3 and g == 1:
                qcols = slice(bi * 128, (bi + 1) * 128)
                sqa = nxt("tmpb", tmpb)
                act_fn(sqa[:, 0:512], ps[:, bo, :], AF.Square)
                bs_ = obank()
                for jj in range(4):
                    mm(ps[:, bs_, 0:128], ones, sqa[:, jj * 128:(jj + 1) * 128], jj == 0, jj == 3)
                i_ = rr.get("rstd", 0)
                rr["rstd"] = i_ + 1
                lv = lnv[i_ % 2][:, 0:128]
                rs = rstd[i_ % 2][:, 0:128]
                act_fn(lv, ps[:, bs_, 0:128], AF.Ln, scale=1.0 / 512, bias=epsc[:, :])
                act_fn(rs, lv, AF.Exp, scale=-0.5)
                st_[("nrm", bi)] = (bo, rs, qcols)

        def b4(bi):
            bo, rs, qcols = st_.pop(("nrm", bi))
            for jj in range(4):
                v_stt(bufA[:, jj, qcols], ps[:, bo, jj * 128:(jj + 1) * 128],
                      gains[:, GC_ATT + jj:GC_ATT + jj + 1], rs, ALU.mult, ALU.mult)

        DEPTH = 3
        nit = len(items)
        a0(items[0])
        a0(items[1])
        for idx in range(nit + DEPTH + 1):
            ia = items[idx] if idx < nit else None
            ib = items[idx - DEPTH] if 0 <= idx - DEPTH < nit else None
            if idx + 2 < nit:
                a0(items[idx + 2])
            if ia:
                a1(ia)
            if ib:
                b1(ib)
            if ia:
                a2(ia)
            if ib:
                b2(ib)
            if ia:
                a3(ia)
            if ib:
                b3(ib)
            il = idx - DEPTH - 1
            if 0 <= il < nit and items[il][1] == 1 and items[il][2] == 3:
                b4(items[il][0])
        assert not st_, st_.keys()
        for tj in range(4):
            wt = wtile(KC * 256)
            wv = wt[:, :].rearrange("p (k n) -> p k n", k=KC)
            for g in range(2):
                src = wmo_d.ap()[g * 256:(g + 1) * 256, tj * 256:(tj + 1) * 256].rearrange("(j d) n -> d j n", d=64)
                wdma(wv[g * 64:(g + 1) * 64, 0:4, :], src)
            wdma(wv[:, 4:8, :], wsrc(wmo_d, 512, 4, tj * 256, 256))
            for mmi in range(2):
                m = tj * 2 + mmi
                for (c0, n) in main_subs:
                    b = bank()
                    for kc in range(KC):
                        mm(ps[:, b, 0:n], wv[:, kc, mmi * 128:(mmi + 1) * 128], bufA[:, kc, c0:c0 + n], kc == 0, kc == KC - 1)
                    v_tt(h[:, m, c0:c0 + n], ps[:, b, 0:n], h[:, m, c0:c0 + n], ALU.add)

    def xattn(main_subs):
        for (c0, n) in main_subs:
            rmsnorm_to_u(h[:, :, c0:c0 + n], u[:, :, c0:c0 + n], n, GC_XATT)
        for tj in range(4):
            wt = wtile(KC * 256)
            wv = wt[:, :].rearrange("p (k n) -> p k n", k=KC)
            wdma(wv, wsrc(wxq_d, 0, KC, tj * 256, 256))
            for mmi in range(2):
                m = tj * 2 + mmi
                for (c0, n) in main_subs:
                    b = bank()
                    for kc in range(KC):
                        mm(ps[:, b, 0:n], wv[:, kc, mmi * 128:(mmi + 1) * 128], u[:, kc, c0:c0 + n], kc == 0, kc == KC - 1)
                    v_copy(bufA[:, m, c0:c0 + n], ps[:, b, 0:n], eng=("act" if mmi else "dve"))
        items = [(c0, n, hx, bi) for (c0, n) in main_subs for hx in range(4) for bi in range(n // 128)]
        st_ = {}

        def xa0(it):
            c0, n, hx, bi = it
            qc = slice(c0 + bi * 128, c0 + (bi + 1) * 128)
            b = sbank()
            for dc in range(2):
                mm(ps[:, b, 0:MEM], bufA[:, 2 * hx + dc, qc], kx[:, 2 * hx + dc, :], dc == 0, dc == 1)
            st_[("S", it)] = b

        def xa1(it):
            b = st_.pop(("S", it))
            sm = nxt("small", small)
            pfb = nxt("pf", pf)
            pbb = nxt("pb", pb)
            st_[it] = (sm, pfb, b, pbb)
            P.op("dve", lambda e: e.tensor_reduce(out=sm[:, 0:1], in_=ps[:, b, 0:MEM], axis=AX.X, op=ALU.max, negate=True),
                 reads=[ps[:, b, 0:MEM]], writes=[sm[:, 0:1]])

        def xa2(it):
            sm, pfb, b, pbb = st_[it]
            v_ts(sm[:, 1:2], sm[:, 0:1], 1.0 / 16, ALU.mult)

        def xa3(it):
            sm, pfb, b, pbb = st_[it]
            act_fn(pfb[:, 0:MEM], ps[:, b, 0:MEM], AF.Exp, scale=1.0 / 16, bias=sm[:, 1:2], accum=sm[:, 2:3])

        def xb1(it):
            sm, pfb, b, pbb = st_[it]
            v_recip(sm[:, 3:4], sm[:, 2:3])

        def xb2(it):
            sm, pfb, b, pbb = st_[it]
            v_ts(pbb[:, 0:MEM], pfb[:, 0:MEM], sm[:, 3:4], ALU.mult)

        def xb3(it):
            c0, n, hx, bi = it
            sm, pfb, b, pbb = st_.pop(it)
            if bi == 0:
                st_[("pT", c0, hx)] = nxt("pT", pT)
            pTx = st_[("pT", c0, hx)]
            bt = obank()
            for mb in range(2):
                mm(ps[:, bt, mb * 128:(mb + 1) * 128], pbb[:, mb * 128:(mb + 1) * 128], identb, True, True)
            v_copy(pTx[:, :, bi * 128:(bi + 1) * 128], ps[:, bt, 0:256].rearrange("p (a c) -> p a c", a=2), eng="act")
            if bi == n // 128 - 1:
                for dc in range(2):
                    b2_ = obank()
                    for mb in range(2):
                        mm(ps[:, b2_, 0:n], vx[:, mb, (2 * hx + dc) * 128:(2 * hx + dc + 1) * 128], pTx[:, mb, 0:n],
                           mb == 0, mb == 1)
                    v_copy(u[:, 2 * hx + dc, c0:c0 + n], ps[:, b2_, 0:n], eng=("act" if dc else "dve"))
                del st_[("pT", c0, hx)]

        DEPTH = 3
        nit = len(items)
        xa0(items[0])
        xa0(items[1])
        for idx in range(nit + DEPTH):
            ia = items[idx] if idx < nit else None
            ib = items[idx - DEPTH] if 0 <= idx - DEPTH < nit else None
            if idx + 2 < nit:
                xa0(items[idx + 2])
            if ia:
                xa1(ia)
            if ib:
                xb1(ib)
            if ia:
                xa2(ia)
            if ib:
                xb2(ib)
            if ia:
                xa3(ia)
            if ib:
                xb3(ib)
        assert not st_
        for tj in range(4):
            wt = wtile(KC * 256)
            wv = wt[:, :].rearrange("p (k n) -> p k n", k=KC)
            wdma(wv, wsrc(wxo_d, 0, KC, tj * 256, 256))
            for mmi in range(2):
                m = tj * 2 + mmi
                for (c0, n) in main_subs:
                    b = bank()
                    for kc in range(KC):
                        mm(ps[:, b, 0:n], wv[:, kc, mmi * 128:(mmi + 1) * 128], u[:, kc, c0:c0 + n], kc == 0, kc == KC - 1)
                    v_tt(h[:, m, c0:c0 + n], ps[:, b, 0:n], h[:, m, c0:c0 + n], ALU.add)

    out_ops = []

    def final_out(p_idx, next_load):
        nrow0 = HALO + (p_idx + 1) * T
        pre = [x_dma(nrow0 + i * 128) for i in range(min(2, NBLK))] if next_load else []
        for bi in range(NBLK):
            stg = nxt("stage", stage)
            bks = [bank(), bank()]
            sm = nxt("small", small)
            for half in range(2):
                for q in range(4):
                    kc = half * 4 + q
                    tr(ps[:, bks[half], q * 128:(q + 1) * 128], h[:, kc, bi * 128:(bi + 1) * 128], identf[:, :])
                act_fn(stg[:, half * 512:(half + 1) * 512], ps[:, bks[half], :], AF.Square, accum=sm[:, half:half + 1])
            v_tt(sm[:, 2:3], sm[:, 0:1], sm[:, 1:2], ALU.add)
            act_fn(sm[:, 3:4], sm[:, 2:3], AF.Ln, scale=1.0 / D, bias=epsc[:, :])
            act_fn(sm[:, 2:3], sm[:, 3:4], AF.Exp, scale=-0.5)
            for half in range(2):
                v_stt(stg[:, half * 512:(half + 1) * 512], ps[:, bks[half], :], sm[:, 2:3],
                      gfin[:, half * 512:(half + 1) * 512], ALU.mult, ALU.mult)
            r0 = p_idx * T + bi * 128
            o = P.dma("sp", out_d.ap()[r0:r0 + 128, :], stg[:, :], reads=[stg[:, :]])
            out_ops.append(o)
            if next_load:
                x_tr(pre[bi], h, bi)
                if bi + 2 < NBLK:
                    pre.append(x_dma(nrow0 + (bi + 2) * 128))


    for p_idx in range(NPASS):
        main_subs = [(s * SUB, SUB) for s in range(NSUB)]
        halo = (p_idx == 0)
        if halo:
            load_x(0, 1, hh)
            load_x(HALO, NBLK, h)
            hook = lambda: rope_tables(0, TP, 0)
        else:
            v_copy(kA[:, 0:128], kA[:, T:T + 128])
            v_copy(kB[:, 0:128], kB[:, T:T + 128])
            v_copy(vtok[:, 0, :], vtok[:, NBLK, :])
            hook = (lambda pp: (lambda: rope_tables(128, T, HALO + pp * T)))(p_idx)
        subs = []
        if halo:
            subs.append((hh, uh, acth, HALO))
        for (c0, n) in main_subs:
            subs.append((h[:, :, c0:c0 + n], u[:, :, c0:c0 + n], act[:, :, c0:c0 + n], n))
        ffn(subs, GC_FFN1, w1i_d, w1o_d, hook=hook)
        mixer(p_idx, main_subs, halo)
        xattn(main_subs)
        subs2 = [(h[:, :, c0:c0 + n], u[:, :, c0:c0 + n], act[:, :, c0:c0 + n], n) for (c0, n) in main_subs]
        ffn(subs2, GC_FFN2, w2i_d, w2o_d)
        final_out(p_idx, p_idx + 1 < NPASS)

    P.finalize(sems)
    block = es.enter_context(nc.Block())

    @block.tensor
    def _(e):
        P.emit("pe", e)

    @block.scalar
    def _(e):
        P.emit("act", e)

    @block.vector
    def _(e):
        P.emit("dve", e)

    @block.gpsimd
    def _(e):
        P.emit("pool", e)

    @block.sync
    def _(e):
        P.emit("sp", e, final_wait=out_ops)

    es.close()
    return nc, P


def _host_consts():
    ident = np.eye(128, dtype=np.float32)
    idx = np.arange(128)
    perm32 = (idx // 64) * 64 + ((idx % 64) + 32) % 64
    p32 = np.zeros((128, 128), np.float32)
    p32[perm32, idx] = 1.0
    perm64 = (idx + 64) % 128
    p64 = np.zeros((128, 128), np.float32)
    p64[perm64, idx] = 1.0
    consts = np.ascontiguousarray(np.stack([ident, p32, p64], axis=1))
    qi = np.arange(128)[:, None]
    ki = np.arange(256)[None, :]
    rel = 128 + qi - ki
    band = (rel >= 0) & (rel < 128)
    m_std = np.where(band, 0.0, NEG).astype(np.float32)
    m_first0 = np.where(band & (ki >= 128), 0.0, NEG).astype(np.float32)
    half = 32
    inv_freq = (np.float32(10000.0) ** (-np.arange(half, dtype=np.float32) / np.float32(half))).astype(np.float32)
    return consts, m_std, m_first0, inv_freq


def make_in_maps(inputs, ntok, ncore_per_seq):
    x = np.asarray(inputs["x"], np.float32)
    mem = np.asarray(inputs["mem"], np.float32)
    pos = np.asarray(inputs["positions"], np.int32)
    B, S, _ = x.shape
    assert S == ntok * ncore_per_seq
    consts, m_std, m_first0, inv_freq = _host_consts()

    def sq(name):
        a = np.asarray(inputs[name], np.float32)
        return np.ascontiguousarray(a[0])

    def colmajor(v):
        v = np.asarray(v, np.float32)
        return v.reshape(-1, 128).T

    gains = np.zeros((128, GCOLS), np.float32)
    gains[:, GC_FFN1:GC_FFN1 + 8] = colmajor(sq("g_ffn1"))
    gains[:, GC_MIX:GC_MIX + 8] = colmajor(sq("g_mix"))
    gains[:, GC_XATT:GC_XATT + 8] = colmajor(sq("g_xattn"))
    gains[:, GC_FFN2:GC_FFN2 + 8] = colmajor(sq("g_ffn2"))
    ga = sq("g_attn_out")
    gains[:, GC_ATT:GC_ATT + 4] = ga.reshape(2, 4, 64).transpose(0, 2, 1).reshape(128, 4)
    gains[:, GC_CONVG:GC_CONVG + 4] = colmajor(sq("g_conv_out"))
    cw = sq("conv_w")
    for c in range(4):
        for tap in range(3):
            gains[:, GC_CONVW + c * 3 + tap] = cw[tap, c * 128:(c + 1) * 128]
    pidx = np.arange(128)
    gains[:, GC_INVF] = inv_freq[pidx % 32]
    gains[:, GC_SSCALE] = np.where((pidx % 64) < 32, -1.0, 1.0)
    gmem_bc = np.ascontiguousarray(np.broadcast_to(sq("g_mem")[None, :], (128, D)))
    gfin_bc = np.ascontiguousarray(np.broadcast_to(np.asarray(inputs["g_final"], np.float32)[None, :], (128, D)))
    sinks_bc = np.ascontiguousarray(np.broadcast_to(sq("sinks")[None, :], (128, 8)))
    shared = {
        "gains": gains, "gmem_bc": gmem_bc, "gfin_bc": gfin_bc, "sinks_bc": sinks_bc, "consts": consts,
        "w_ffn1_in": sq("w_ffn1_in"), "w_ffn1_out": sq("w_ffn1_out"), "w_mix_in": sq("w_mix_in"),
        "w_mix_out": sq("w_mix_out"), "w_xq": sq("w_xq"), "w_xkv": sq("w_xkv"), "w_xo": sq("w_xo"),
        "w_ffn2_in": sq("w_ffn2_in"), "w_ffn2_out": sq("w_ffn2_out"),
    }
    in_maps = []
    for b in range(B):
        for hf in range(ncore_per_seq):
            s0 = hf * ntok
            xc = np.zeros((HALO + ntok, D), np.float32)
            pc = np.zeros((HALO + ntok,), np.int32)
            xc[HALO:] = x[b, s0:s0 + ntok]
            pc[HALO:] = pos[b, s0:s0 + ntok]
            if hf > 0:
                xc[:HALO] = x[b, s0 - HALO:s0]
                pc[:HALO] = pos[b, s0 - HALO:s0]
            masks = np.ascontiguousarray(np.stack([m_first0 if hf == 0 else m_std, m_std], axis=1))
            m = dict(shared)
            m.update({"x": xc, "pos": np.ascontiguousarray(np.broadcast_to(pc[None, :], (128, HALO + ntok))),
                      "mem": np.ascontiguousarray(mem[b]), "masks": masks})
            in_maps.append(m)
    return in_maps


_NC_CACHE = {}


def kernel(**inputs):
    x = np.asarray(inputs["x"])
    B, S, _ = x.shape
    ncore = 8
    per_seq = ncore // B
    ntok = S // per_seq
    cfg = Cfg(ntok=ntok, T=1024, SUB=512)
    key = (ntok,)
    if key not in _NC_CACHE:
        _NC_CACHE[key] = build_program(cfg)[0]
    nc = _NC_CACHE[key]
    in_maps = make_in_maps(inputs, ntok, per_seq)
    res = run_bass_kernel_spmd(nc, in_maps, core_ids=list(range(ncore)))
    out = np.empty((B, S, D), np.float32)
    i = 0
    for b in range(B):
        for hf in range(per_seq):
            out[b, hf * ntok:(hf + 1) * ntok] = res.results[i]["out"]
            i += 1
    return out
```
